# Optimizing a Trainium2 kernel written in Bass

```python
import jax, jax.numpy as jnp
from jax import lax
import numpy as np

D_MODEL = 1024
BATCH = 8
SEQ = 4096
DEPTH = 2
DEC_BATCH = 32
DEC_SEQ = 64
PAST_LEN = 1024

CHUNK = 64
HEAD_DIM = 64
H_A = 8
Q_LORA = 768
KV_LORA = 256
NOPE_DIM = 64
ROPE_DIM = 32
V_DIM = 64
ROPE_THETA = 10000.0
MLA_SCALE = (NOPE_DIM + ROPE_DIM) ** -0.5
H_B = 8
PREV_CHUNKS = 8
BAND_PAST = PREV_CHUNKS * CHUNK
REL_MAX = 256
N_REL = REL_MAX + CHUNK
BAND_SCALE = HEAD_DIM ** -0.5
H_C = 16
SB_SCALE = HEAD_DIM ** -0.5
Q_BLOCK = 128
IN_AB = Q_LORA + KV_LORA + ROPE_DIM + 3 * H_B * HEAD_DIM
MIX_AB = H_A * V_DIM + H_B * HEAD_DIM
IN_C = 3 * H_C * HEAD_DIM
MIX_C = H_C * HEAD_DIM
D_FF = 4 * D_MODEL
N_EVEN = (DEPTH + 1) // 2
N_ODD = DEPTH // 2
ALPHA = (2.0 * DEPTH) ** 0.25
BETA = (8.0 * DEPTH) ** -0.25
NEG_INF = -1e30

kernel_name = "hybrid_mla_band_stickbreak_stream_step"


def _layer_norm(x, g, b, eps=1e-5):
    xf = x.astype(jnp.float32)
    mu = jnp.mean(xf, -1, keepdims=True)
    var = jnp.mean(jnp.square(xf - mu), -1, keepdims=True)
    return ((xf - mu) * lax.rsqrt(var + eps) * g.astype(jnp.float32) + b.astype(jnp.float32)).astype(x.dtype)


def _rms_norm(x, g, eps=1e-6):
    xf = x.astype(jnp.float32)
    return (xf * lax.rsqrt(jnp.mean(jnp.square(xf), -1, keepdims=True) + eps) * g.astype(jnp.float32)).astype(x.dtype)


def _rope(x, pos):
    half = ROPE_DIM // 2
    inv = jnp.power(ROPE_THETA, -jnp.arange(half, dtype=jnp.float32) / half)
    ang = pos.astype(jnp.float32)[:, None] * inv[None, :]
    ang = ang.reshape((pos.shape[0],) + (1,) * (x.ndim - 3) + (half,))
    cos, sin = jnp.cos(ang), jnp.sin(ang)
    xf = x.astype(jnp.float32)
    x1, x2 = xf[..., :half], xf[..., half:]
    return jnp.concatenate([x1 * cos - x2 * sin, x2 * cos + x1 * sin], -1).astype(x.dtype)


def _chunk_mask(q_pos, k_pos):
    return (k_pos[None, :] // CHUNK) <= (q_pos[:, None] // CHUNK)


def _softmax_attend(q, k, v, mask, scale):
    s = jnp.einsum('bqhd,bkhd->bhqk', q, k).astype(jnp.float32) * scale
    p = jax.nn.softmax(jnp.where(mask, s, NEG_INF), axis=-1)
    return jnp.einsum('bhqk,bkhd->bqhd', p.astype(v.dtype), v)


def _stick_breaking(q, k, v, q_pos, k_pos):
    z = jnp.einsum('bqhd,bkhd->bhqk', q, k).astype(jnp.float32) * SB_SCALE
    mask = k_pos[None, :] < q_pos[:, None]
    log_1m = jnp.where(mask, jax.nn.log_sigmoid(-z), 0.0)
    tail = lax.cumsum(log_1m, axis=z.ndim - 1, reverse=True) - log_1m
    w = jnp.where(mask, jnp.exp(jax.nn.log_sigmoid(z) + tail), 0.0)
    return jnp.einsum('bhqk,bkhd->bqhd', w.astype(v.dtype), v)


def _blockwise(fn, q, q_pos):
    B, S = q.shape[:2]
    nb = S // Q_BLOCK
    qb = jnp.moveaxis(q.reshape((B, nb, Q_BLOCK) + q.shape[2:]), 1, 0)
    pb = q_pos.reshape(nb, Q_BLOCK)
    out = lax.map(lambda a: fn(a[0], a[1]), (qb, pb))
    return jnp.moveaxis(out, 0, 1).reshape((B, S) + out.shape[3:])


def _rel_index(dist):
    return jnp.clip(dist, -(CHUNK - 1), REL_MAX) + (CHUNK - 1)


def _mla_keys(ckv, kr, w_ukv):
    B, T = ckv.shape[:2]
    kv = (ckv @ w_ukv).reshape(B, T, H_A, NOPE_DIM + V_DIM)
    k = jnp.concatenate([kv[..., :NOPE_DIM], jnp.broadcast_to(kr[:, :, None, :], (B, T, H_A, ROPE_DIM))], -1)
    return k, kv[..., NOPE_DIM:]


def _even_project(x, pos, w_in, g_q, w_uq, g_kv):
    B, S, _ = x.shape
    h = x @ w_in
    o1 = Q_LORA
    o2 = o1 + KV_LORA
    o3 = o2 + ROPE_DIM
    q_a = (_rms_norm(h[..., :o1], g_q) @ w_uq).reshape(B, S, H_A, NOPE_DIM + ROPE_DIM)
    q_a = jnp.concatenate([q_a[..., :NOPE_DIM], _rope(q_a[..., NOPE_DIM:], pos)], -1)
    ckv = _rms_norm(h[..., o1:o2], g_kv)
    kr = _rope(h[..., o2:o3], pos)
    hb = h[..., o3:].reshape(B, S, 3, H_B, HEAD_DIM)
    return q_a, ckv, kr, hb[:, :, 0], hb[:, :, 1], hb[:, :, 2]


def _band_prompt(q, k, v, table):
    B, S, H, d = q.shape
    n_c = S // CHUNK
    n_band = PREV_CHUNKS + 1
    qc = q.reshape(B, n_c, CHUNK, H, d)
    pad = ((0, 0), (PREV_CHUNKS, 0), (0, 0), (0, 0), (0, 0))
    kc = jnp.pad(k.reshape(B, n_c, CHUNK, H, d), pad)
    vc = jnp.pad(v.reshape(B, n_c, CHUNK, H, d), pad)
    s = jnp.stack([jnp.einsum('bnihd,bnjhd->bhnij', qc, kc[:, PREV_CHUNKS - o:PREV_CHUNKS - o + n_c])
                   for o in range(n_band)], axis=-2).astype(jnp.float32) * BAND_SCALE
    off = jnp.arange(n_band)
    i = jnp.arange(CHUNK)
    dist = off[None, :, None] * CHUNK + i[:, None, None] - i[None, None, :]
    bias = table[:, _rel_index(dist)].astype(jnp.float32)
    valid = jnp.arange(n_c)[:, None] >= off[None, :]
    s = jnp.where(valid[None, None, :, None, :, None], s + bias[None, :, None], NEG_INF)
    p = jax.nn.softmax(s.reshape(B, H, n_c, CHUNK, n_band * CHUNK), axis=-1).reshape(s.shape).astype(v.dtype)
    out = sum(jnp.einsum('bhnij,bnjhd->bnihd', p[..., o, :], vc[:, PREV_CHUNKS - o:PREV_CHUNKS - o + n_c])
              for o in range(n_band))
    return out.reshape(B, S, H, d)


def _band_sample(q, k_all, v_all, n_past, table):
    T = q.shape[1]
    k_rel = jnp.arange(k_all.shape[1]) - n_past
    dist = jnp.arange(T)[:, None] - k_rel[None, :]
    bias = table[:, _rel_index(dist)].astype(jnp.float32)
    s = jnp.einsum('bqhd,bkhd->bhqk', q, k_all).astype(jnp.float32) * BAND_SCALE + bias[None]
    p = jax.nn.softmax(s, axis=-1)
    return jnp.einsum('bhqk,bkhd->bqhd', p.astype(v_all.dtype), v_all)


def _even_prompt(x, pos, w_in, g_q, w_uq, g_kv, w_ukv, table, w_out):
    B, S, _ = x.shape
    q_a, ckv, kr, q_b, k_b, v_b = _even_project(x, pos, w_in, g_q, w_uq, g_kv)
    k_a, v_a = _mla_keys(ckv, kr, w_ukv)
    o_a = _blockwise(lambda qb, pb: _softmax_attend(qb, k_a, v_a, _chunk_mask(pb, pos), MLA_SCALE), q_a, pos)
    o_b = _band_prompt(q_b, k_b, v_b, table)
    y = jnp.concatenate([o_a.reshape(B, S, H_A * V_DIM), o_b.reshape(B, S, H_B * HEAD_DIM)], -1) @ w_out
    rows = min(BAND_PAST, S)
    return y, ckv, kr, k_b[:, S - rows:], v_b[:, S - rows:]


def _even_sample(x, pos, c_ckv, c_kr, c_bk, c_bv, w_in, g_q, w_uq, g_kv, w_ukv, table, w_out):
    B, T, _ = x.shape
    q_a, ckv, kr, q_b, k_b, v_b = _even_project(x, pos, w_in, g_q, w_uq, g_kv)
    ckv_all = jnp.concatenate([c_ckv, ckv], 1)
    kr_all = jnp.concatenate([c_kr, kr], 1)
    k_a, v_a = _mla_keys(ckv_all, kr_all, w_ukv)
    o_a = _softmax_attend(q_a, k_a, v_a, _chunk_mask(pos, jnp.arange(ckv_all.shape[1])), MLA_SCALE)
    kb_all = jnp.concatenate([c_bk, k_b], 1)
    vb_all = jnp.concatenate([c_bv, v_b], 1)
    o_b = _band_sample(q_b, kb_all, vb_all, c_bk.shape[1], table)
    y = jnp.concatenate([o_a.reshape(B, T, H_A * V_DIM), o_b.reshape(B, T, H_B * HEAD_DIM)], -1) @ w_out
    rows = min(BAND_PAST, kb_all.shape[1])
    return y, ckv, kr, kb_all[:, kb_all.shape[1] - rows:], vb_all[:, vb_all.shape[1] - rows:]


def _odd_project(x, w_in):
    B, S, _ = x.shape
    h = (x @ w_in).reshape(B, S, 3, H_C, HEAD_DIM)
    return h[:, :, 0], h[:, :, 1], h[:, :, 2]


def _odd_prompt(x, pos, w_in, w_out):
    B, S, _ = x.shape
    q, k, v = _odd_project(x, w_in)
    o = _blockwise(lambda qb, pb: _stick_breaking(qb, k, v, pb, pos), q, pos)
    return o.reshape(B, S, MIX_C) @ w_out, k, v


def _odd_sample(x, pos, c_k, c_v, w_in, w_out):
    B, T, _ = x.shape
    q, k, v = _odd_project(x, w_in)
    k_all = jnp.concatenate([c_k, k], 1)
    v_all = jnp.concatenate([c_v, v], 1)
    o = _stick_breaking(q, k_all, v_all, pos, jnp.arange(k_all.shape[1]))
    return o.reshape(B, T, MIX_C) @ w_out, k, v


def _post_block(x, mix, g1, b1, g2, b2, w_up, w_down):
    x = _layer_norm(ALPHA * x + mix, g1, b1)
    ff = jnp.square(jax.nn.relu(x @ w_up)) @ w_down
    return _layer_norm(ALPHA * x + ff, g2, b2)


def setup_inputs(seed: int = 0) -> dict:
    key = jax.random.key(seed)
    ks = jax.random.split(key, 24)

    def nrm(k, shape, scale):
        return jax.random.normal(k, shape, jnp.float32) * scale

    band_rows = min(BAND_PAST, PAST_LEN)
    return {
        "x_prompt": nrm(ks[0], (BATCH, SEQ, D_MODEL), 1.0),
        "x_sample": nrm(ks[1], (DEC_BATCH, DEC_SEQ, D_MODEL), 1.0),
        "cache_mla_ckv": nrm(ks[2], (N_EVEN, DEC_BATCH, PAST_LEN, KV_LORA), 1.0),
        "cache_mla_krope": nrm(ks[3], (N_EVEN, DEC_BATCH, PAST_LEN, ROPE_DIM), 1.0),
        "cache_band_k": nrm(ks[4], (N_EVEN, DEC_BATCH, band_rows, H_B, HEAD_DIM), 1.0),
        "cache_band_v": nrm(ks[5], (N_EVEN, DEC_BATCH, band_rows, H_B, HEAD_DIM), 1.0),
        "cache_sb_k": nrm(ks[6], (N_ODD, DEC_BATCH, PAST_LEN, H_C, HEAD_DIM), 1.0),
        "cache_sb_v": nrm(ks[7], (N_ODD, DEC_BATCH, PAST_LEN, H_C, HEAD_DIM), 1.0),
        "w_in_ab": nrm(ks[8], (N_EVEN, D_MODEL, IN_AB), D_MODEL ** -0.5),
        "g_q_lat": 1.0 + nrm(ks[9], (N_EVEN, Q_LORA), 0.02),
        "w_uq": nrm(ks[10], (N_EVEN, Q_LORA, H_A * (NOPE_DIM + ROPE_DIM)), Q_LORA ** -0.5),
        "g_kv_lat": 1.0 + nrm(ks[11], (N_EVEN, KV_LORA), 0.02),
        "w_ukv": nrm(ks[12], (N_EVEN, KV_LORA, H_A * (NOPE_DIM + V_DIM)), KV_LORA ** -0.5),
        "rel_bias": nrm(ks[13], (N_EVEN, H_B, N_REL), 0.1),
        "w_out_ab": nrm(ks[14], (N_EVEN, MIX_AB, D_MODEL), BETA * MIX_AB ** -0.5),
        "w_in_c": nrm(ks[15], (N_ODD, D_MODEL, IN_C), D_MODEL ** -0.5),
        "w_out_c": nrm(ks[16], (N_ODD, MIX_C, D_MODEL), BETA * MIX_C ** -0.5),
        "ln_mix_g": 1.0 + nrm(ks[17], (DEPTH, D_MODEL), 0.02),
        "ln_mix_b": nrm(ks[18], (DEPTH, D_MODEL), 0.02),
        "ln_ffn_g": 1.0 + nrm(ks[19], (DEPTH, D_MODEL), 0.02),
        "ln_ffn_b": nrm(ks[20], (DEPTH, D_MODEL), 0.02),
        "w_ff_up": nrm(ks[21], (DEPTH, D_MODEL, D_FF), D_MODEL ** -0.5),
        "w_ff_down": nrm(ks[22], (DEPTH, D_FF, D_MODEL), BETA * D_FF ** -0.5),
    }


def reference(x_prompt, x_sample, cache_mla_ckv, cache_mla_krope, cache_band_k, cache_band_v,
              cache_sb_k, cache_sb_v, w_in_ab, g_q_lat, w_uq, g_kv_lat, w_ukv, rel_bias, w_out_ab,
              w_in_c, w_out_c, ln_mix_g, ln_mix_b, ln_ffn_g, ln_ffn_b, w_ff_up, w_ff_down):
    pos_p = jnp.arange(x_prompt.shape[1], dtype=jnp.int32)
    n_past = cache_mla_ckv.shape[2]
    pos_s = n_past + jnp.arange(x_sample.shape[1], dtype=jnp.int32)
    xp, xs = x_prompt, x_sample
    p_ckv, p_kr, p_bk, p_bv, p_sk, p_sv = [], [], [], [], [], []
    s_ckv, s_kr, s_bk, s_bv, s_sk, s_sv = [], [], [], [], [], []
    for l in range(DEPTH):
        i = l // 2
        if l % 2 == 0:
            mp, a1, a2, a3, a4 = _even_prompt(xp, pos_p, w_in_ab[i], g_q_lat[i], w_uq[i], g_kv_lat[i],
                                              w_ukv[i], rel_bias[i], w_out_ab[i])
            ms, b1, b2, b3, b4 = _even_sample(xs, pos_s, cache_mla_ckv[i], cache_mla_krope[i],
                                              cache_band_k[i], cache_band_v[i], w_in_ab[i], g_q_lat[i],
                                              w_uq[i], g_kv_lat[i], w_ukv[i], rel_bias[i], w_out_ab[i])
            p_ckv.append(a1); p_kr.append(a2); p_bk.append(a3); p_bv.append(a4)
            s_ckv.append(b1); s_kr.append(b2); s_bk.append(b3); s_bv.append(b4)
        else:
            mp, a1, a2 = _odd_prompt(xp, pos_p, w_in_c[i], w_out_c[i])
            ms, b1, b2 = _odd_sample(xs, pos_s, cache_sb_k[i], cache_sb_v[i], w_in_c[i], w_out_c[i])
            p_sk.append(a1); p_sv.append(a2)
            s_sk.append(b1); s_sv.append(b2)
        xp = _post_block(xp, mp, ln_mix_g[l], ln_mix_b[l], ln_ffn_g[l], ln_ffn_b[l], w_ff_up[l], w_ff_down[l])
        xs = _post_block(xs, ms, ln_mix_g[l], ln_mix_b[l], ln_ffn_g[l], ln_ffn_b[l], w_ff_up[l], w_ff_down[l])
    return (xp, xs,
            jnp.stack(p_ckv), jnp.stack(p_kr), jnp.stack(p_bk), jnp.stack(p_bv), jnp.stack(p_sk), jnp.stack(p_sv),
            jnp.stack(s_ckv), jnp.stack(s_kr), jnp.stack(s_bk), jnp.stack(s_bv), jnp.stack(s_sk), jnp.stack(s_sv))
```

```python
import numpy as np
import ml_dtypes
import concourse.bass as bass
import concourse.mybir as mybir
from concourse.bass_utils import run_bass_kernel_spmd

F32 = mybir.dt.float32
BF16 = mybir.dt.bfloat16
AF = mybir.ActivationFunctionType
ALU = mybir.AluOpType

D = 1024
SEQ = 4096
NPT = 32
NT = 34
NTOK = NT * 128
PAST = 1024
ALPHA = 4.0 ** 0.25
MLA_SCALE = 96.0 ** -0.5
LN_EPS = 1e-5
RMS_EPS = 1e-6
NMB = 66
NBB = 50


class Buf:
    __slots__ = ("name", "lw", "rd", "dead", "excl")

    def __init__(self, name="", excl=False):
        self.name = name
        self.lw = None
        self.rd = {}
        self.dead = False
        self.excl = excl


class Rec:
    __slots__ = ("eng", "emit", "deps", "sig", "pos", "key", "is_dep", "cclk", "waits",
                 "isdma", "sigidx", "gid")


class Sched:
    CH = 30000
    DCH = 1800

    def __init__(self, nc):
        self.nc = nc
        self.q = {e: [] for e in ("pe", "act", "dve", "pool", "sp")}
        self.all = []
        self.nslots = {"sp": 16, "pool": 8, "act": 4}
        self.slot_rr = {k: 0 for k in self.nslots}
        self.slot_last = {}
        self.slot_cnt = {}
        self._cc = {e: 0 for e in self.q}

    def _add(self, eng, emit, reads, writes, isdma=False):
        r = Rec()
        r.eng = eng
        r.emit = emit
        r.sig = False
        r.is_dep = False
        r.isdma = isdma
        r.cclk = None
        r.gid = len(self.all)
        deps = {}
        for b in reads:
            assert not b.dead, f"read of dead buffer {b.name}"
            if b.lw is not None:
                deps[b.lw.gid] = b.lw
            if b.excl:
                for x in b.rd.values():
                    if x.eng != eng:
                        deps[x.gid] = x
        for b in writes:
            assert not b.dead, f"write of dead buffer {b.name}"
            if b.lw is not None:
                deps[b.lw.gid] = b.lw
            for x in b.rd.values():
                deps[x.gid] = x
        if isdma:
            sl = self.slot_rr[eng]
            self.slot_rr[eng] = (sl + 1) % self.nslots[eng]
            key = ("d", eng, sl)
            prev = self.slot_last.get(key)
            if prev is not None:
                deps[prev.gid] = prev
            self.slot_last[key] = r
            self.slot_cnt[key] = self.slot_cnt.get(key, 0) + 1
            r.key = key
            r.pos = self.slot_cnt[key]
        else:
            r.key = ("c", eng)
            r.pos = self._cc[eng] + 1
        deps.pop(r.gid, None)
        dl = []
        for g in sorted(deps.keys(), reverse=True):
            d = deps[g]
            if (not isdma) and eng == "pe" and d.key == ("c", "pe"):
                continue
            d.is_dep = True
            dl.append(d)
        r.deps = dl
        for b in reads:
            b.rd[r.key] = r
        for b in writes:
            b.lw = r
            b.rd = {}
        self.q[eng].append(r)
        self.all.append(r)
        if not isdma:
            self._cc[eng] = r.pos
        return r

    def op(self, eng, emit, reads=(), writes=()):
        return self._add(eng, emit, reads, writes, False)

    def dma(self, queue, out, in_, reads=(), writes=(), **kw):
        return self._add(queue, lambda e: e.dma_start(out=out, in_=in_, **kw), reads, writes, True)

    def barrier(self):
        lasts = []
        for e, lst in self.q.items():
            for r in reversed(lst):
                if not r.isdma:
                    lasts.append(r)
                    break
        lasts += list(self.slot_last.values())
        bb = Buf("barrier")
        recs = []
        for e in ("pe", "act", "dve", "pool", "sp"):
            r = self._add(e, lambda en: en.nop(), [], [], False)
            dl = {d.gid: d for d in lasts}
            for d in r.deps:
                dl[d.gid] = d
            r.deps = [dl[g] for g in sorted(dl.keys(), reverse=True)]
            for d in r.deps:
                d.is_dep = True
            recs.append(r)
        return recs

    def finalize(self, stack):
        nc = self.nc
        clocks = {e: {} for e in self.q}
        for r in self.all:
            clk = clocks[r.eng]
            waits = []
            for d in r.deps:
                if clk.get(d.key, 0) >= d.pos:
                    continue
                waits.append(d)
                d.sig = True
                if d.cclk:
                    for k, v in d.cclk.items():
                        if clk.get(k, 0) < v:
                            clk[k] = v
                if clk.get(d.key, 0) < d.pos:
                    clk[d.key] = d.pos
            r.waits = waits
            if r.is_dep:
                r.cclk = dict(clk)
        nsig = {}
        for e, lst in self.q.items():
            k = 0
            for r in lst:
                if r.isdma:
                    continue
                if r.sig:
                    r.sigidx = k
                    k += 1
            nsig[e] = k
        self.csems = {}
        for e, k in nsig.items():
            n = max(1, (k + self.CH - 1) // self.CH)
            self.csems[e] = [stack.enter_context(nc.semaphore(f"c_{e}_{i}")) for i in range(n)]
        self.dsems = {}
        for key, cnt in self.slot_cnt.items():
            n = (cnt + self.DCH - 1) // self.DCH
            self.dsems[key] = [stack.enter_context(nc.semaphore(f"d_{key[1]}_{key[2]}_{i}")) for i in range(n)]

        def semval(d):
            if d.isdma:
                k = d.pos - 1
                return self.dsems[d.key][k // self.DCH], 16 * (k % self.DCH + 1)
            k = d.sigidx
            return self.csems[d.eng][k // self.CH], (k % self.CH) + 1

        def emit_engine(ename, e):
            for r in self.q[ename]:
                seen = {}
                for d in r.waits:
                    s, v = semval(d)
                    kk = id(s)
                    if kk not in seen or seen[kk][1] < v:
                        seen[kk] = (s, v)
                for s, v in seen.values():
                    e.wait_ge(s, v)
                ins = r.emit(e)
                if r.isdma:
                    s, v = semval(r)
                    ins.then_inc(s, 16)
                elif r.sig:
                    s, v = semval(r)
                    ins.then_inc(s, 1)
            if ename == "sp":
                for key, r in self.slot_last.items():
                    s, v = semval(r)
                    e.wait_ge(s, v)

        with nc.Block() as block:
            @block.tensor
            def _(e):
                emit_engine("pe", e)

            @block.scalar
            def _(e):
                emit_engine("act", e)

            @block.vector
            def _(e):
                emit_engine("dve", e)

            @block.gpsimd
            def _(e):
                emit_engine("pool", e)

            @block.sync
            def _(e):
                emit_engine("sp", e)


class Arena:
    BASE = 16512
    LIMIT = 229344

    def __init__(self, nc):
        self.nc = nc
        self.off = self.BASE
        self.n = 0

    def alloc(self, name, shape, dt):
        sz = 4 if dt == F32 else 2
        n = 1
        for s in shape[1:]:
            n *= s
        nb = (n * sz + 31) // 32 * 32
        assert self.off + nb <= self.LIMIT, f"SBUF overflow allocating {name}: {self.off}+{nb}"
        self.n += 1
        t = self.nc.alloc_sbuf_tensor_at(f"{name}_{self.n}", list(shape), dt, offset=self.off)
        self.off += nb
        return t

    def mark(self):
        return self.off

    def reset(self, m):
        self.off = m


class Ring:
    def __init__(self, arena, name, shape, dt, n):
        self.items = [(arena.alloc(f"{name}{i}", shape, dt), Buf(f"{name}{i}")) for i in range(n)]
        self.i = 0

    def next(self):
        it = self.items[self.i]
        self.i = (self.i + 1) % len(self.items)
        return it


DBG = {}


def build_program(stop_after=99):
    nc = bass.Bass("TRN2", target_bir_lowering=False)
    S = Sched(nc)
    A = Arena(nc)

    def din(name, shape, dt=F32):
        return nc.dram_tensor(name, list(shape), dt, kind="ExternalInput").ap()

    def dout(name, shape, dt=F32):
        return nc.dram_tensor(name, list(shape), dt, kind="ExternalOutput").ap()

    def dscr(name, shape, dt):
        return nc.dram_tensor(name, list(shape), dt, kind="Internal").ap()

    xin = din("xin", [NTOK, D])
    rope = din("rope", [NTOK, 256])
    c_ckv = din("c_ckv", [4 * PAST, 256])
    c_kr = din("c_kr", [4 * PAST, 32])
    c_bk = din("c_bk", [4 * 512, 512])
    c_bv = din("c_bv", [4 * 512, 512])
    c_sk = din("c_sk", [4 * PAST, 1024])
    c_sv = din("c_sv", [4 * PAST, 1024])
    w_in_ab = din("w_in_ab", [D, 2592])
    g_q = din("g_q", [128, 6])
    w_uq = din("w_uq", [768, 768])
    g_kv = din("g_kv", [1, 256])
    w_ukv = din("w_ukv", [256, 1024])
    rel_bias = din("rel_bias", [8, 320])
    w_out_ab = din("w_out_ab", [1024, 1024])
    w_in_c = din("w_in_c", [D, 3072])
    w_out_c = din("w_out_c", [1024, 1024])
    ln_p = din("ln_p", [8, 1024])
    w_up = din("w_up", [2, D, 4096])
    w_dn = din("w_dn", [2, 4096, D])
    consts = din("consts", [128, 5, 128])
    y_o = dout("y", [NTOK, D])
    ckv_o = dout("ckv_o", [NTOK, 256])
    kr_o = dout("kr_o", [NTOK, 32])
    bkp_o = dout("bkp_o", [512, 512])
    bvp_o = dout("bvp_o", [512, 512])
    bks_o = dout("bks_o", [4 * 512, 512])
    bvs_o = dout("bvs_o", [4 * 512, 512])
    sk_o = dout("sk_o", [NTOK, 1024])
    sv_o = dout("sv_o", [NTOK, 1024])
    qat_d = dscr("qat_d", [NT, 96, 1024], BF16)
    kat_d = dscr("kat_d", [NMB, 96, 1024], BF16)
    va_d = dscr("va_d", [NMB, 128, 520], BF16)
    bqt_d = dscr("bqt_d", [NT, 128, 512], BF16)
    bkt_d = dscr("bkt_d", [NBB, 128, 512], BF16)
    bv_d = dscr("bv_d", [NBB, 128, 520], BF16)
    xmid_d = dscr("xmid_d", [NTOK, D], F32)
    x1_d = dscr("x1_d", [NTOK, D], F32)
    sqt_d = dscr("sqt_d", [NT, 128, 1024], BF16)
    skt_d = dscr("skt_d", [NMB, 128, 1024], BF16)
    sv_d = dscr("sv_d", [NMB, 128, 1024], BF16)
    tp_d = dscr("tp_d", [8, 768], F32)
    wb_out_ab = dscr("wb_out_ab", [1024, 1024], BF16)
    wb_up = dscr("wb_up", [2, D, 4096], BF16)
    wb_dn = dscr("wb_dn", [2, 4096, D], BF16)
    wb_in_c = dscr("wb_in_c", [D, 3072], BF16)
    wb_out_c = dscr("wb_out_c", [1024, 1024], BF16)

    dumps = []

    def dump(name, ap, shape, dt, bufs):
        if not DBG.get('dump'):
            return
        o = nc.dram_tensor("dbg_" + name, list(shape), dt, kind="ExternalOutput").ap()
        S.dma("sp", o, ap, list(bufs), [])
        dumps.append(name)

    Bd = {}

    def db(*k):
        if k not in Bd:
            Bd[k] = Buf(str(k))
        return Bd[k]

    banks = []
    for i in range(8):
        t = nc.alloc_psum_tensor(f"ps{i}", [128, 512], F32)
        banks.append((t, Buf(f"ps{i}", excl=True)))
    bank_i = [0]
    rot_pool = [list(range(8))]

    def psum_at(i):
        t, old = banks[i]
        nb = Buf(old.name, excl=True)
        nb.lw = old.lw
        nb.rd = old.rd
        old.dead = True
        banks[i] = (t, nb)
        return t, nb

    def psum():
        pool = rot_pool[0]
        bank_i[0] = (bank_i[0] + 1) % len(pool)
        return psum_at(pool[bank_i[0]])

    identf = A.alloc("identf", [128, 128], F32)
    identb = A.alloc("identb", [128, 128], BF16)
    negtri = A.alloc("negtri", [128, 128], BF16)
    negones = A.alloc("negones", [128, 128], BF16)
    ones_c = A.alloc("ones_c", [128, 1], BF16)
    Bc = Buf("consts")
    cst = A.alloc("cst", [128, 5, 128], F32)
    e127 = A.alloc("e127", [128, 128], BF16)
    S.dma("sp", cst[:, :, :], consts[:, :, :], [], [Bc])
    S.op("dve", lambda e: e.tensor_copy(out=identf[:, :], in_=cst[:, 0, :]), [Bc], [Bc])
    S.op("dve", lambda e: e.tensor_copy(out=identb[:, :], in_=cst[:, 0, :]), [Bc], [Bc])
    S.op("dve", lambda e: e.tensor_copy(out=negtri[:, :], in_=cst[:, 1, :]), [Bc], [Bc])
    S.op("dve", lambda e: e.tensor_copy(out=negones[:, :], in_=cst[:, 2, :]), [Bc], [Bc])
    S.op("dve", lambda e: e.memset(ones_c[:, :], 1.0), [], [Bc])
    S.op("dve", lambda e: e.tensor_copy(out=e127[:, :], in_=cst[:, 4, :]), [Bc], [Bc])
    P0 = A.mark()
    lnbc_cur = [None, None]

    def load_lnbc(gi):
        t = A.alloc("lnbc", [128, 2, 1024], F32)
        b = Buf("lnbc")
        for i in range(2):
            S.dma("sp", t[:, i, :], ln_p[gi + i:gi + i + 1, :].broadcast_to([128, 1024]), [], [b])
        lnbc_cur[0] = t
        lnbc_cur[1] = b

    def transposes_f32(src_t, src_b, n_rows, dstT, dst_b, nchunks):
        for g in range(0, nchunks, 4):
            ps, pb = psum()
            m = min(4, nchunks - g)
            for i in range(m):
                S.op("pe", lambda e, ps=ps, i=i, g=g: e.transpose(
                    out=ps[:, i * n_rows:(i + 1) * n_rows],
                    in_=src_t[0:n_rows, (g + i) * 128:(g + i + 1) * 128],
                    identity=identf[0:n_rows, 0:n_rows]), [src_b, Bc], [pb])
            S.op("act", lambda e, ps=ps, g=g, m=m: e.copy(
                out=dstT[:, g:g + m, 0:n_rows],
                in_=ps[:, 0:m * n_rows].rearrange("p (c n) -> p c n", c=m)), [pb], [dst_b])

    def transposes_bf(src_ap_fn, src_b, n_rows, width, n, dst_ap_fn, dst_b, eng="act"):
        for g in range(0, n, 8):
            ps, pb = psum()
            pv = ps[:, :].bitcast(BF16)
            m = min(8, n - g)
            for i in range(m):
                S.op("pe", lambda e, pv=pv, i=i, g=g: e.transpose(
                    out=pv[0:width, i * n_rows:(i + 1) * n_rows],
                    in_=src_ap_fn(g + i),
                    identity=identb[0:n_rows, 0:n_rows]), [src_b, Bc], [pb])
            if eng == "act":
                S.op("act", lambda e, pv=pv, g=g, m=m: e.copy(
                    out=dst_ap_fn(g, m),
                    in_=pv[0:width, 0:m * n_rows].rearrange("p (c n) -> p c n", c=m)), [pb], [dst_b])
            else:
                S.op("dve", lambda e, pv=pv, g=g, m=m: e.tensor_copy(
                    out=dst_ap_fn(g, m),
                    in_=pv[0:width, 0:m * n_rows].rearrange("p (c n) -> p c n", c=m)), [pb], [dst_b])

    def layer_norm(v_t, v_b, n_rows, gi, out_t, out_b, tmp_t, tmp_b, st_t, st_b):
        S.op("dve", lambda e: e.bn_stats(out=st_t[0:n_rows, 0:6], in_=v_t[0:n_rows, 0:512]), [v_b], [st_b])
        S.op("dve", lambda e: e.bn_stats(out=st_t[0:n_rows, 6:12], in_=v_t[0:n_rows, 512:1024]), [v_b], [st_b])
        S.op("dve", lambda e: e.bn_aggr(out=st_t[0:n_rows, 12:14], in_=st_t[0:n_rows, 0:12]), [st_b], [st_b])
        S.op("dve", lambda e: e.tensor_scalar(out=st_t[0:n_rows, 14:15], in0=st_t[0:n_rows, 13:14],
                                              scalar1=LN_EPS, scalar2=None, op0=ALU.add), [st_b], [st_b])
        S.op("act", lambda e: e.activation(out=st_t[0:n_rows, 14:15], in_=st_t[0:n_rows, 14:15], func=AF.Ln),
             [st_b], [st_b])
        S.op("act", lambda e: e.activation(out=st_t[0:n_rows, 15:16], in_=st_t[0:n_rows, 14:15], func=AF.Exp,
                                           scale=-0.5), [st_b], [st_b])
        S.op("dve", lambda e: e.tensor_scalar(out=tmp_t[0:n_rows, :], in0=v_t[0:n_rows, :],
                                              scalar1=st_t[0:n_rows, 12:13], scalar2=st_t[0:n_rows, 15:16],
                                              op0=ALU.subtract, op1=ALU.mult), [v_b, st_b], [tmp_b])
        lnbc, lnb = lnbc_cur
        S.op("pool", lambda e: e.tensor_tensor(out=tmp_t[0:n_rows, :], in0=tmp_t[0:n_rows, :],
                                               in1=lnbc[0:n_rows, 0, :], op=ALU.mult), [tmp_b, lnb], [tmp_b])
        S.op("pool", lambda e: e.tensor_tensor(out=out_t[0:n_rows, :], in0=tmp_t[0:n_rows, :],
                                               in1=lnbc[0:n_rows, 1, :], op=ALU.add), [tmp_b, lnb], [out_b])

    def load_w(dst, src_ap, b):
        S.dma("pool", dst, src_ap, [], [b], max_dma_last_dim=4096)

    W = Buf("wA0")
    w_in_sb = A.alloc("w_in", [128, 8, 2592], BF16)
    for j in range(8):
        load_w(w_in_sb[:, j, :], w_in_ab[128 * j:128 * (j + 1), :], W)
    w_uq_f = A.alloc("w_uq_f", [128, 6, 768], F32)
    w_uq_sb = A.alloc("w_uq", [128, 6, 768], BF16)
    gq_sb = A.alloc("gq", [128, 6], F32)
    S.dma("sp", gq_sb[:, :], g_q[:, :], [], [W])
    S.dma("sp", w_uq_f[:, :, :], w_uq.rearrange("(c p) n -> p c n", p=128), [], [W])
    for j in range(6):
        S.op("pool", lambda e, j=j: e.tensor_scalar(out=w_uq_sb[:, j, :], in0=w_uq_f[:, j, :],
                                                    scalar1=gq_sb[:, j:j + 1], scalar2=None, op0=ALU.mult),
             [W], [W])
    w_ukv_sb = A.alloc("w_ukv", [128, 2, 1024], BF16)
    load_w(w_ukv_sb[:, :, :], w_ukv.rearrange("(c p) n -> p c n", p=128), W)
    gkv_bc = A.alloc("gkv_bc", [128, 256], F32)
    S.dma("sp", gkv_bc[:, :], g_kv[0:1, :].broadcast_to([128, 256]), [], [W])

    if not DBG.get("nobg"):
        S.dma("pool", wb_out_ab[:, :], w_out_ab[:, :], [], [db("wbf", "out_ab")], max_dma_last_dim=4096)
        S.dma("pool", wb_up[0], w_up[0], [], [db("wbf", "up0")], max_dma_last_dim=4096)
        S.dma("pool", wb_dn[0], w_dn[0], [], [db("wbf", "dn0")], max_dma_last_dim=4096)
        S.dma("pool", wb_in_c[:, :], w_in_c[:, :], [], [db("wbf", "in_c")], max_dma_last_dim=4096)
        S.dma("pool", wb_out_c[:, :], w_out_c[:, :], [], [db("wbf", "out_c")], max_dma_last_dim=4096)
        S.dma("pool", wb_up[1], w_up[1], [], [db("wbf", "up1")], max_dma_last_dim=4096)
        S.dma("pool", wb_dn[1], w_dn[1], [], [db("wbf", "dn1")], max_dma_last_dim=4096)

    def load_wb(dst, src_ap, key, b):
        S.dma("sp", dst, src_ap, [db("wbf", key)], [b])

    xt_r = Ring(A, "xt", [128, 1024], F32, 2)
    rp_r = Ring(A, "rp", [128, 256], F32, 2)
    xT_r = Ring(A, "xT", [128, 8, 128], BF16, 2)
    hqb_r = Ring(A, "hqb", [128, 768], BF16, 2)
    hqT_r = Ring(A, "hqT", [128, 6, 128], BF16, 2)
    junk = A.alloc("junk", [128, 512], BF16)
    Bjunk = Buf("junk")
    ss_r = Ring(A, "ss", [128, 8], F32, 2)
    bst_r = Ring(A, "bst", [128, 24], F32, 2)
    qf_r = Ring(A, "qf", [128, 8, 96], F32, 2)
    qb_r = Ring(A, "qb", [128, 8, 96], BF16, 2)
    rt_r = Ring(A, "rt", [128, 4, 8, 16], F32, 2)
    QT_r = Ring(A, "QT", [96, 8, 128], BF16, 2)
    ckvf_r = Ring(A, "ckvf", [128, 256], F32, 3)
    ckvb_r = Ring(A, "ckvb", [128, 256], BF16, 2)
    ckvT_r = Ring(A, "ckvT", [128, 2, 128], BF16, 2)
    krf_r = Ring(A, "krf", [128, 32], F32, 3)
    hkvf_r = Ring(A, "hkvf", [128, 256], F32, 2)
    Kf_r = Ring(A, "Kf", [128, 8, 96], BF16, 2)
    KT_r = Ring(A, "KT", [96, 8, 128], BF16, 2)
    Va_r = [(A.alloc(f"Va{i}", [128, 8, 65], BF16), Buf(f"Va{i}")) for i in range(2)]
    for t, b in Va_r:
        S.op("pool", lambda e, t=t: e.memset(t[:, :, 64:65], 1.0), [], [b])
    va_i = [0]
    bqb_r = Ring(A, "bqb", [128, 512], BF16, 2)
    BqT_r = Ring(A, "BqT", [128, 4, 128], BF16, 2)
    bkf_r = Ring(A, "bkf", [128, 512], F32, 3)
    bvf_r = Ring(A, "bvf", [128, 512], F32, 3)
    bkb_r = Ring(A, "bkb", [128, 512], BF16, 2)
    BkT_r = Ring(A, "BkT", [128, 4, 128], BF16, 2)
    Bv_r = [(A.alloc(f"Bv{i}", [128, 8, 65], BF16), Buf(f"Bv{i}")) for i in range(2)]
    for t, b in Bv_r:
        S.op("pool", lambda e, t=t: e.memset(t[:, :, 64:65], 1.0), [], [b])
    bv_i = [0]

    def mla_kv(ckvb, ckvb_b, krf, krf_b, blk):
        ckvT, ckvT_b = ckvT_r.next()
        transposes_bf(lambda i: ckvb[:, 128 * i:128 * (i + 1)], ckvb_b, 128, 128, 2,
                      lambda g, m: ckvT[:, g:g + m, :], ckvT_b)
        Kf, Kf_b = Kf_r.next()
        Vt, Vt_b = Va_r[va_i[0]]
        va_i[0] ^= 1
        for n in range(2):
            ps, pb = psum()
            for j in range(2):
                S.op("pe", lambda e, ps=ps, j=j, n=n: e.matmul(
                    ps[:, 0:512], lhsT=ckvT[:, j, :], rhs=w_ukv_sb[:, j, 512 * n:512 * (n + 1)],
                    start=(j == 0), stop=(j == 1)), [ckvT_b, W], [pb])
            pv = ps[:, 0:512].rearrange("p (h c) -> p h c", h=4)
            S.op("act", lambda e, pv=pv, n=n: e.copy(out=Kf[:, 4 * n:4 * n + 4, 0:64], in_=pv[:, :, 0:64]),
                 [pb], [Kf_b])
            S.op("dve", lambda e, pv=pv, n=n: e.tensor_copy(out=Vt[:, 4 * n:4 * n + 4, 0:64],
                                                            in_=pv[:, :, 64:128]), [pb], [Vt_b])
        S.op("pool", lambda e: e.tensor_copy(out=Kf[:, :, 64:96],
                                             in_=krf[:, 0:32].unsqueeze(1).broadcast_to([128, 8, 32])),
             [krf_b], [Kf_b])
        KT, KT_b = KT_r.next()
        transposes_bf(lambda i: Kf[:, i, :], Kf_b, 128, 96, 8, lambda g, m: KT[:, g:g + m, :], KT_b, eng="dve")
        S.dma("sp", kat_d[blk], KT[:, :, :].rearrange("p h n -> p (h n)"), [KT_b], [db("kat", blk)])
        S.dma("sp", va_d[blk], Vt[:, :, :].rearrange("p h n -> p (h n)"), [Vt_b], [db("va", blk)])

    def band_kv(k_src, k_b, v_src, v_b, blk, from_psum):
        bkb, bkb_b = bkb_r.next()
        S.op("dve", lambda e: e.tensor_copy(out=bkb[:, :], in_=k_src), [k_b], [bkb_b])
        BkT, BkT_b = BkT_r.next()
        transposes_bf(lambda i: bkb[:, 128 * i:128 * (i + 1)], bkb_b, 128, 128, 4,
                      lambda g, m: BkT[:, g:g + m, :], BkT_b)
        S.dma("sp", bkt_d[blk], BkT[:, :, :].rearrange("p h n -> p (h n)"), [BkT_b], [db("bkt", blk)])
        Bv, Bv_b = Bv_r[bv_i[0]]
        bv_i[0] ^= 1
        eng = "dve" if from_psum else "pool"
        S.op(eng, lambda e: e.tensor_copy(out=Bv[:, :, 0:64],
                                          in_=v_src.rearrange("p (h c) -> p h c", h=8)), [v_b], [Bv_b])
        S.dma("sp", bv_d[blk], Bv[:, :, :].rearrange("p h n -> p (h n)"), [Bv_b], [db("bv", blk)])

    CH_A = [(0, 512), (512, 1024), (1024, 1056), (1056, 1568), (1568, 2080), (2080, 2592)]

    def tile_A0(nt):
        r0 = 128 * nt
        xt, xt_b = xt_r.next()
        S.dma("sp", xt[:, :], xin[r0:r0 + 128, :], [], [xt_b])
        rp, rp_b = rp_r.next()
        S.dma("sp", rp[:, :], rope[r0:r0 + 128, :], [], [rp_b])
        xT, xT_b = xT_r.next()
        transposes_f32(xt, xt_b, 128, xT, xT_b, 8)
        if DBG.get('cut', 99) <= 1:
            return
        def chunk(ci):
            c0, c1 = CH_A[ci]
            ps, pb = psum()
            for j in range(8):
                S.op("pe", lambda e, j=j: e.matmul(
                    ps[:, 0:c1 - c0], lhsT=xT[:, j, :], rhs=w_in_sb[:, j, c0:c1],
                    start=(j == 0), stop=(j == 7)), [xT_b, W], [pb])
            return ps, pb
        p0, b0 = chunk(0)
        p1, b1 = chunk(1)
        ss, ss_b = ss_r.next()
        if DBG.get('sub') == 1:
            return
        bst, bst_b = bst_r.next()
        S.op("dve", lambda e: e.bn_stats(out=bst[:, 0:6], in_=p0[:, 0:512]), [b0], [bst_b])
        S.op("dve", lambda e: e.bn_stats(out=bst[:, 6:12], in_=p1[:, 0:256]), [b1], [bst_b])
        S.op("dve", lambda e: e.bn_stats(out=bst[:, 12:18], in_=p1[:, 256:512]), [b1], [bst_b])
        S.op("dve", lambda e: e.bn_aggr(out=bst[:, 18:20], in_=bst[:, 0:12]), [bst_b], [bst_b])
        S.op("dve", lambda e: e.bn_aggr(out=bst[:, 20:22], in_=bst[:, 12:18]), [bst_b], [bst_b])
        S.op("dve", lambda e: e.scalar_tensor_tensor(out=ss[:, 3:4], in0=bst[:, 18:19], scalar=bst[:, 18:19],
                                                     in1=bst[:, 19:20], op0=ALU.mult, op1=ALU.add),
             [bst_b], [ss_b])
        S.op("dve", lambda e: e.scalar_tensor_tensor(out=ss[:, 6:7], in0=bst[:, 20:21], scalar=bst[:, 20:21],
                                                     in1=bst[:, 21:22], op0=ALU.mult, op1=ALU.add),
             [bst_b], [ss_b])
        if DBG.get('sub') == 2:
            return
        hqb, hqb_b = hqb_r.next()
        S.op("dve", lambda e: e.tensor_copy(out=hqb[:, 0:512], in_=p0[:, 0:512]), [b0], [hqb_b])
        S.op("dve", lambda e: e.tensor_copy(out=hqb[:, 512:768], in_=p1[:, 0:256]), [b1], [hqb_b])
        if DBG.get('sub') == 3:
            return
        hkvf, hkvf_b = hkvf_r.next()
        S.op("act", lambda e: e.activation(out=hkvf[:, :], in_=p1[:, 256:512], func=AF.Copy), [b1], [hkvf_b])
        if DBG.get('cut', 99) <= 2:
            return
        p2, b2 = chunk(2)
        rp_k, rp_kb = rp, rp_b
        krf, krf_b = krf_r.next()
        rt2, rt2_b = rt_r.next()
        k1 = rt2[:, 0, 0, :]
        k2 = rt2[:, 1, 0, :]
        k3 = rt2[:, 2, 0, :]
        k4 = rt2[:, 3, 0, :]
        S.op("dve", lambda e: e.tensor_tensor(out=k1, in0=p2[:, 0:16], in1=rp[:, 0:16], op=ALU.mult),
             [b2, rp_b], [rt2_b])
        S.op("dve", lambda e: e.tensor_tensor(out=k2, in0=p2[:, 16:32], in1=rp[:, 128:144], op=ALU.mult),
             [b2, rp_b], [rt2_b])
        S.op("dve", lambda e: e.tensor_tensor(out=k3, in0=p2[:, 16:32], in1=rp[:, 0:16], op=ALU.mult),
             [b2, rp_b], [rt2_b])
        S.op("dve", lambda e: e.tensor_tensor(out=k4, in0=p2[:, 0:16], in1=rp[:, 128:144], op=ALU.mult),
             [b2, rp_b], [rt2_b])
        S.op("dve", lambda e: e.tensor_tensor(out=krf[:, 0:16], in0=k1, in1=k2, op=ALU.subtract),
             [rt2_b], [krf_b])
        S.op("dve", lambda e: e.tensor_tensor(out=krf[:, 16:32], in0=k3, in1=k4, op=ALU.add),
             [rt2_b], [krf_b])
        S.dma("sp", kr_o[r0:r0 + 128, :], krf[:, :], [krf_b], [])
        if DBG.get('cut', 99) <= 3:
            return
        p3, b3 = chunk(3)
        bqb, bqb_b = bqb_r.next()
        S.op("dve", lambda e: e.tensor_scalar(out=bqb[:, :], in0=p3[:, 0:512], scalar1=0.125, scalar2=None,
                                              op0=ALU.mult), [b3], [bqb_b])
        p4, b4 = chunk(4)
        p5, b5 = chunk(5)
        need_out = (nt >= 28)
        if need_out:
            bkf, bkf_b = bkf_r.next()
            bvf, bvf_b = bvf_r.next()
            S.op("act", lambda e: e.activation(out=bkf[:, :], in_=p4[:, 0:512], func=AF.Copy), [b4], [bkf_b])
            S.op("act", lambda e: e.activation(out=bvf[:, :], in_=p5[:, 0:512], func=AF.Copy), [b5], [bvf_b])
            if nt < NPT:
                o0 = 128 * (nt - 28)
                S.dma("sp", bkp_o[o0:o0 + 128, :], bkf[:, :], [bkf_b], [])
                S.dma("sp", bvp_o[o0:o0 + 128, :], bvf[:, :], [bvf_b], [])
            else:
                for hs in range(2):
                    sq = 2 * (nt - NPT) + hs
                    S.dma("sp", bks_o[512 * sq + 448:512 * sq + 512, :], bkf[64 * hs:64 * hs + 64, :], [bkf_b], [])
                    S.dma("sp", bvs_o[512 * sq + 448:512 * sq + 512, :], bvf[64 * hs:64 * hs + 64, :], [bvf_b], [])
        band_kv(p4[:, 0:512], b4, p5[:, 0:512], b5, nt, True)
        BqT, BqT_b = BqT_r.next()
        transposes_bf(lambda i: bqb[:, 128 * i:128 * (i + 1)], bqb_b, 128, 128, 4,
                      lambda g, m: BqT[:, g:g + m, :], BqT_b)
        S.dma("sp", bqt_d[nt], BqT[:, :, :].rearrange("p h n -> p (h n)"), [BqT_b], [db("bqt", nt)])
        if DBG.get('cut', 99) <= 4:
            return
        S.op("dve", lambda e: e.tensor_scalar(out=ss[:, 3:4], in0=ss[:, 3:4], scalar1=RMS_EPS, scalar2=None,
                                              op0=ALU.add), [ss_b], [ss_b])
        S.op("dve", lambda e: e.tensor_scalar(out=ss[:, 6:7], in0=ss[:, 6:7], scalar1=RMS_EPS, scalar2=None,
                                              op0=ALU.add), [ss_b], [ss_b])
        S.op("act", lambda e: e.activation(out=ss[:, 3:4], in_=ss[:, 3:4], func=AF.Ln), [ss_b], [ss_b])
        S.op("act", lambda e: e.activation(out=ss[:, 4:5], in_=ss[:, 3:4], func=AF.Exp, scale=-0.5),
             [ss_b], [ss_b])
        S.op("act", lambda e: e.activation(out=ss[:, 6:7], in_=ss[:, 6:7], func=AF.Ln), [ss_b], [ss_b])
        S.op("act", lambda e: e.activation(out=ss[:, 5:6], in_=ss[:, 6:7], func=AF.Exp, scale=-0.5),
             [ss_b], [ss_b])
        hqT, hqT_b = hqT_r.next()
        transposes_bf(lambda i: hqb[:, 128 * i:128 * (i + 1)], hqb_b, 128, 128, 6,
                      lambda g, m: hqT[:, g:g + m, :], hqT_b)
        qf, qf_b = qf_r.next()
        qf2 = qf[:, :, :].rearrange("p h c -> p (h c)")
        for (n0, n1) in ((0, 512), (512, 768)):
            ps, pb = psum()
            for j in range(6):
                S.op("pe", lambda e, ps=ps, j=j, n0=n0, n1=n1: e.matmul(
                    ps[:, 0:n1 - n0], lhsT=hqT[:, j, :], rhs=w_uq_sb[:, j, n0:n1],
                    start=(j == 0), stop=(j == 5)), [hqT_b, W], [pb])
            S.op("act", lambda e, ps=ps, n0=n0, n1=n1: e.activation(
                out=qf2[:, n0:n1], in_=ps[:, 0:n1 - n0], func=AF.Copy, scale=ss[:, 4:5]), [pb, ss_b], [qf_b])
        qb, qb_b = qb_r.next()
        rt, rt_b = rt_r.next()
        cos8 = rp[:, 0:128].rearrange("p (h c) -> p h c", h=8)
        sin8 = rp[:, 128:256].rearrange("p (h c) -> p h c", h=8)
        S.op("pool", lambda e: e.tensor_copy(out=qb[:, :, 0:64], in_=qf[:, :, 0:64]), [qf_b], [qb_b])
        S.op("dve", lambda e: e.tensor_tensor(out=rt[:, 0], in0=qf[:, :, 64:80], in1=cos8, op=ALU.mult),
             [qf_b, rp_b], [rt_b])
        S.op("dve", lambda e: e.tensor_tensor(out=rt[:, 1], in0=qf[:, :, 80:96], in1=sin8, op=ALU.mult),
             [qf_b, rp_b], [rt_b])
        S.op("dve", lambda e: e.tensor_tensor(out=rt[:, 2], in0=qf[:, :, 80:96], in1=cos8, op=ALU.mult),
             [qf_b, rp_b], [rt_b])
        S.op("dve", lambda e: e.tensor_tensor(out=rt[:, 3], in0=qf[:, :, 64:80], in1=sin8, op=ALU.mult),
             [qf_b, rp_b], [rt_b])
        S.op("dve", lambda e: e.tensor_tensor(out=qb[:, :, 64:80], in0=rt[:, 0], in1=rt[:, 1], op=ALU.subtract),
             [rt_b], [qb_b])
        S.op("dve", lambda e: e.tensor_tensor(out=qb[:, :, 80:96], in0=rt[:, 2], in1=rt[:, 3], op=ALU.add),
             [rt_b], [qb_b])
        QT, QT_b = QT_r.next()
        transposes_bf(lambda i: qb[:, i, :], qb_b, 128, 96, 8, lambda g, m: QT[:, g:g + m, :], QT_b)
        S.dma("sp", qat_d[nt], QT[:, :, :].rearrange("p h n -> p (h n)"), [QT_b], [db("qat", nt)])
        if DBG.get('cut', 99) <= 5:
            return
        ckvf, ckvf_b = ckvf_r.next()
        S.op("dve", lambda e: e.scalar_tensor_tensor(out=ckvf[:, :], in0=hkvf[:, :], scalar=ss[:, 5:6],
                                                     in1=gkv_bc[:, :], op0=ALU.mult, op1=ALU.mult),
             [hkvf_b, ss_b, W], [ckvf_b])
        S.dma("sp", ckv_o[r0:r0 + 128, :], ckvf[:, :], [ckvf_b], [])
        ckvb, ckvb_b = ckvb_r.next()
        S.op("pool", lambda e: e.tensor_copy(out=ckvb[:, :], in_=ckvf[:, :]), [ckvf_b], [ckvb_b])
        mla_kv(ckvb, ckvb_b, krf, krf_b, nt)

    for nt in DBG.get('tilesA', range(NT)):
        tile_A0(nt)

    def past_mla(sq, j):
        r0 = PAST * sq + 128 * j
        ckvf, ckvf_b = ckvf_r.next()
        S.dma("sp", ckvf[:, :], c_ckv[r0:r0 + 128, :], [], [ckvf_b])
        krf, krf_b = krf_r.next()
        S.dma("sp", krf[:, :], c_kr[r0:r0 + 128, :], [], [krf_b])
        ckvb, ckvb_b = ckvb_r.next()
        S.op("pool", lambda e: e.tensor_copy(out=ckvb[:, :], in_=ckvf[:, :]), [ckvf_b], [ckvb_b])
        mla_kv(ckvb, ckvb_b, krf, krf_b, 34 + 8 * sq + j)

    def past_band(sq, j):
        r0 = 512 * sq + 128 * j
        bkf, bkf_b = bkf_r.next()
        bvf, bvf_b = bvf_r.next()
        S.dma("sp", bkf[:, :], c_bk[r0:r0 + 128, :], [], [bkf_b])
        S.dma("sp", bvf[:, :], c_bv[r0:r0 + 128, :], [], [bvf_b])
        band_kv(bkf[:, :], bkf_b, bvf[:, :], bvf_b, 34 + 4 * sq + j, False)

    for sq in DBG.get('seqs', range(4)):
        for j in DBG.get('pastj', range(8)):
            past_mla(sq, j)
        for j in DBG.get('pastjb', range(4)):
            past_band(sq, j)
        S.dma("sp", bks_o[512 * sq:512 * sq + 448, :], c_bk[512 * sq + 64:512 * sq + 512, :], [], [])
        S.dma("sp", bvs_o[512 * sq:512 * sq + 448, :], c_bv[512 * sq + 64:512 * sq + 512, :], [], [])

    if stop_after <= 0:
        return nc, S

    def groups_of(blocks, n=4):
        out = []
        for b in blocks:
            if out and len(out[-1]) < n and out[-1][-1][1] == b[1]:
                out[-1].append(b)
            else:
                out.append([b])
        return out

    def out_proj_ln(Ob, Ob_b, nq, w_sb, w_b, xr, xr_b, rows0, dst_d, dst_key, wk):
        OT, OT_b = wk["OT"].next()
        transposes_bf(lambda i: Ob[0:nq, 128 * i:128 * (i + 1)], Ob_b, nq, 128, 8,
                      lambda g, m: OT[:, g:g + m, 0:nq], OT_b)
        v, v_b = wk["v"].next()
        for n in range(2):
            ps, pb = psum()
            for j in range(8):
                S.op("pe", lambda e, ps=ps, j=j, n=n: e.matmul(
                    ps[0:nq, 0:512], lhsT=OT[:, j, 0:nq], rhs=w_sb[:, j, 512 * n:512 * (n + 1)],
                    start=(j == 0), stop=(j == 7)), [OT_b, w_b], [pb])
            S.op("dve", lambda e, ps=ps, n=n: e.scalar_tensor_tensor(
                out=v[0:nq, 512 * n:512 * (n + 1)], in0=xr[0:nq, 512 * n:512 * (n + 1)], scalar=ALPHA,
                in1=ps[0:nq, 0:512], op0=ALU.mult, op1=ALU.add), [pb, xr_b], [v_b])
        xm, xm_b = wk["xm"].next()
        tmp, tmp_b = wk["tmp"].next()
        st, st_b = wk["st"].next()
        layer_norm(v, v_b, nq, 0, xm, xm_b, tmp, tmp_b, st, st_b)
        S.dma("sp", dst_d[rows0:rows0 + nq, :], xm[0:nq, :], [xm_b], [db(dst_key, rows0 // 128)])

    def seq_defs():
        sd = [dict(kind="p", qtiles=DBG.get("qtiles", list(range(NPT))))]
        for sq in DBG.get("seqs", range(4)):
            sd.append(dict(kind="s", sq=sq))
        if DBG.get("noprompt"):
            sd = sd[1:]
        return sd

    S.barrier()
    A.reset(P0)
    rot_pool[0] = [2, 3, 4, 5, 6, 7]
    load_lnbc(0)
    WB = Buf("wB0")
    w_out_sb = A.alloc("w_out", [128, 8, 1024], BF16)
    for j in range(8):
        load_wb(w_out_sb[:, j, :], wb_out_ab[128 * j:128 * (j + 1), :], "out_ab", WB)
    tpad = A.alloc("tpad", [8, 768], F32)
    rbt = A.alloc("rbt", [8, 320], F32)
    Btp = Buf("tpad")
    S.dma("sp", rbt[:, :], rel_bias[:, :], [], [Btp])
    S.op("dve", lambda e: e.tensor_copy(out=tpad[:, 64:384], in_=rbt[:, :]), [Btp], [Btp])
    S.op("dve", lambda e: e.tensor_copy(out=tpad[:, 0:64], in_=rbt[:, 0:1].broadcast_to([8, 64])), [Btp], [Btp])
    S.op("dve", lambda e: e.tensor_copy(out=tpad[:, 384:768], in_=rbt[:, 319:320].broadcast_to([8, 384])),
         [Btp], [Btp])
    S.dma("sp", tp_d[:, :], tpad[:, :], [Btp], [db("tp")])
    biasf = A.alloc("biasf", [128, 3, 8, 128], F32)
    biasb = A.alloc("biasb", [128, 3, 8, 128], BF16)
    Bbf = Buf("biasf")
    for m in range(3):
        for h in range(8):
            src = bass.AP(tensor=tp_d.tensor, offset=h * 768 + 128 * m, ap=[[1, 128], [1, 128]])
            S.dma("sp", biasf[:, m, h, :], src, [db("tp")], [Bbf])

    def unreverse(m, h4):
        ps, pb = psum()
        for i in range(4):
            S.op("pe", lambda e, i=i: e.matmul(ps[:, 128 * i:128 * (i + 1)], lhsT=cst[:, 3, :],
                                               rhs=biasf[:, m, 4 * h4 + i, :], start=True, stop=True),
                 [Bbf, Bc], [pb])
        S.op("dve", lambda e: e.tensor_copy(out=biasb[:, m, 4 * h4:4 * h4 + 4, :].rearrange("p a b -> p (a b)"),
                                            in_=ps[:, 0:512]), [pb], [WB])
    for m in range(3):
        for h4 in range(2):
            unreverse(m, h4)
    cb = A.alloc("cb", [128, 8], F32)
    S.dma("sp", cb[:, :], bass.AP(tensor=tp_d.tensor, offset=767, ap=[[0, 128], [768, 8]]), [db("tp")], [WB],
          allow_slow_non_contiguous=True)

    if DBG.get('b0cut') == 1:
        return nc, S
    KT_sb = A.alloc("KT_sb", [96, 32, 8, 128], BF16)
    V_sb = A.alloc("V_sb", [128, 32, 520], BF16)
    kbuf = [Buf(f"kb{i}") for i in range(32)]
    BK_sb = A.alloc("BK_sb", [128, 8, 4, 128], BF16)
    BV_sb = A.alloc("BV_sb", [128, 8, 520], BF16)
    bbuf = [Buf(f"bb{i}") for i in range(8)]
    wkB = dict(
        QT=Ring(A, "QTs", [96, 8, 128], BF16, 2), BQ=Ring(A, "BQz", [128, 8, 128], BF16, 2),
        xr=Ring(A, "xr", [128, 1024], F32, 2), PT=Ring(A, "PT", [128, 512], BF16, 3),
        Ob=Ring(A, "Ob", [128, 1024], BF16, 2), rs=Ring(A, "rs", [128, 16], F32, 2),
        OT=Ring(A, "OT", [128, 8, 128], BF16, 2), v=Ring(A, "v", [128, 1024], F32, 1),
        xm=Ring(A, "xm", [128, 1024], F32, 2), tmp=Ring(A, "tmp", [128, 1024], F32, 1),
        st=Ring(A, "st", [128, 16], F32, 2),
    )

    for t_, b_ in wkB["BQ"].items:
        S.op("pool", lambda e, t_=t_: e.memset(t_[:, :, :], 0.0), [], [b_])

    def load_seq_B0(sd):
        if sd["kind"] == "p":
            for b in range(max(sd["qtiles"]) + 1):
                S.dma("sp", KT_sb[:, b, :, :].rearrange("p h n -> p (h n)"), kat_d[b], [db("kat", b)], [kbuf[b]])
                S.dma("sp", V_sb[:, b, :], va_d[b], [db("va", b)], [kbuf[b]])
        else:
            sq = sd["sq"]
            for j in range(8):
                b = 34 + 8 * sq + j
                S.dma("sp", KT_sb[:, j, :, :].rearrange("p h n -> p (h n)"), kat_d[b], [db("kat", b)], [kbuf[j]])
                S.dma("sp", V_sb[:, j, :], va_d[b], [db("va", b)], [kbuf[j]])
            b = 32 + sq // 2
            off = 64 * (sq % 2)
            S.op("pool", lambda e: e.memset(KT_sb[:, 8, :, 64:128], 0.0), [], [kbuf[8]])
            S.op("pool", lambda e: e.memset(V_sb[64:128, 8, :], 0.0), [], [kbuf[8]])
            S.dma("sp", KT_sb[:, 8, :, 0:64], kat_d[b].rearrange("p (h n) -> p h n", h=8)[:, :, off:off + 64],
                  [db("kat", b)], [kbuf[8]])
            S.dma("sp", V_sb[0:64, 8, :], va_d[b][off:off + 64, :], [db("va", b)], [kbuf[8]])
            for j in range(4):
                b = 34 + 4 * sq + j
                S.dma("sp", BK_sb[:, j, :, :].rearrange("p h n -> p (h n)"), bkt_d[b], [db("bkt", b)], [bbuf[j]])
                S.dma("sp", BV_sb[:, j, :], bv_d[b], [db("bv", b)], [bbuf[j]])
            b = 32 + sq // 2
            S.op("pool", lambda e: e.memset(BK_sb[:, 4, :, 64:128], 0.0), [], [bbuf[4]])
            S.op("pool", lambda e: e.memset(BV_sb[64:128, 4, :], 0.0), [], [bbuf[4]])
            S.dma("sp", BK_sb[:, 4, :, 0:64], bkt_d[b].rearrange("p (h n) -> p h n", h=4)[:, :, off:off + 64],
                  [db("bkt", b)], [bbuf[4]])
            S.dma("sp", BV_sb[0:64, 4, :], bv_d[b][off:off + 64, :], [db("bv", b)], [bbuf[4]])

    def attn_tile_B0(sd, tq):
        prompt = sd["kind"] == "p"
        nq = 128 if prompt else 64
        tile_id = tq if prompt else 32 + sd["sq"] // 2
        off = 0 if prompt else 64 * (sd["sq"] % 2)
        rows0 = 128 * tile_id + off
        QT, QT_b = wkB["QT"].next()
        S.dma("sp", QT[:, :, 0:nq], qat_d[tile_id].rearrange("p (h n) -> p h n", h=8)[:, :, off:off + nq],
              [db("qat", tile_id)], [QT_b])
        BQ, BQ_b = wkB["BQ"].next()
        bq_src = bqt_d[tile_id].rearrange("p (h n) -> p h n", h=4)
        BQv = BQ[:, :, :].rearrange("p (c two) n -> p c two n", two=2)
        S.dma("sp", BQv[0:64, :, 0, 0:nq], bq_src[0:64, :, off:off + nq], [db("bqt", tile_id)], [BQ_b])
        S.dma("sp", BQv[64:128, :, 1, 0:nq], bq_src[64:128, :, off:off + nq], [db("bqt", tile_id)], [BQ_b])
        xr, xr_b = wkB["xr"].next()
        S.dma("sp", xr[0:nq, :], xin[rows0:rows0 + nq, :], [], [xr_b])
        if prompt:
            sl = tq % 8
            S.dma("sp", BK_sb[:, sl, :, :].rearrange("p h n -> p (h n)"), bkt_d[tq], [db("bkt", tq)], [bbuf[sl]])
            S.dma("sp", BV_sb[:, sl, :], bv_d[tq], [db("bv", tq)], [bbuf[sl]])
            kblocks = [(b, 128, b == tq) for b in range(tq + 1)]
        else:
            kblocks = [(j, 128, False) for j in range(8)] + [(8, 128, "pad")]
        Ob, Ob_b = wkB["Ob"].next()
        rs, rs_b = wkB["rs"].next()

        def normalize(pso, col0):
            for h in range(8):
                po, pob = pso[h // 4]
                c0 = (h % 4) * 65
                S.op("dve", lambda e, po=po, c0=c0, h=h: e.reciprocal(
                    out=rs[0:nq, col0 // 64 + h:col0 // 64 + h + 1], in_=po[0:nq, c0 + 64:c0 + 65]), [pob], [rs_b])
                S.op("dve", lambda e, po=po, c0=c0, h=h: e.tensor_scalar(
                    out=Ob[0:nq, col0 + 64 * h:col0 + 64 * h + 64], in0=po[0:nq, c0:c0 + 64],
                    scalar1=rs[0:nq, col0 // 64 + h:col0 // 64 + h + 1], scalar2=None, op0=ALU.mult),
                     [pob, rs_b], [Ob_b])

        pso = [psum_at(0), psum_at(1)]
        grps = groups_of(kblocks)
        def mla_A(it):
            h, grp, gi = it["h"], it["grp"], it["gi"]
            ps, pb = psum()
            kn = grp[0][1]
            n = len(grp)
            for j, (b, _, dg) in enumerate(grp):
                S.op("pe", lambda e, j=j, b=b: e.matmul(
                    ps[0:kn, j * nq:(j + 1) * nq], lhsT=KT_sb[:, b, h, 0:kn], rhs=QT[:, h, 0:nq],
                    start=True, stop=True), [kbuf[b], QT_b], [pb])
            PT, PT_b = wkB["PT"].next()
            it["PT"], it["PT_b"] = PT, PT_b
            S.op("act", lambda e: e.activation(
                out=PT[0:kn, 0:n * nq], in_=ps[0:kn, 0:n * nq], func=AF.Exp, scale=MLA_SCALE), [pb], [PT_b])
            for j, (b, _, dg) in enumerate(grp):
                if dg == "pad":
                    S.op("pool", lambda e, j=j: e.memset(PT[64:128, j * nq:(j + 1) * nq], 0.0), [PT_b], [PT_b])
                elif dg:
                    S.op("pool", lambda e, j=j: e.memset(PT[64:128, j * nq:j * nq + 64], 0.0), [PT_b], [PT_b])

        def mla_B(it, pso):
            h, grp = it["h"], it["grp"]
            po, pob = pso[h // 4]
            c0 = (h % 4) * 65
            kn = grp[0][1]
            PT, PT_b = it["PT"], it["PT_b"]
            for j, (b, _, dg) in enumerate(grp):
                idx = it["base"] + j
                S.op("pe", lambda e, j=j, b=b, st=(idx == 0), sp=(idx == it["nblk"] - 1): e.matmul(
                    po[0:nq, c0:c0 + 65], lhsT=PT[0:kn, j * nq:(j + 1) * nq],
                    rhs=V_sb[0:kn, b, h * 65:(h + 1) * 65], start=st, stop=sp), [PT_b, kbuf[b]], [pob])

        items = []
        for h in range(8):
            base = 0
            for gi, grp in enumerate(grps):
                items.append(dict(h=h, grp=grp, gi=gi, base=base, nblk=len(kblocks)))
                base += len(grp)
        for i in range(len(items) + 1):
            if i < len(items):
                mla_A(items[i])
            if i >= 1:
                mla_B(items[i - 1], pso)
        normalize(pso, 0)
        if DBG.get('b0cut') == 2:
            return
        if prompt:
            bl = [(tq - m, 128, m) for m in range(5) if tq - m >= 0]
            gA = [(b % 8, kn, m) for (b, kn, m) in bl if m < 3]
            gB = [(b % 8, kn, m) for (b, kn, m) in bl if m >= 3]
            bgrps = [g for g in (gA, gB) if g]
        else:
            bgrps = [[(0, 128, 3), (1, 128, 3)], [(2, 128, 2), (3, 128, 1), (4, 128, 0)]]
        pso = [psum_at(0), psum_at(1)]
        nblk = sum(len(g) for g in bgrps)
        def band_A(it):
            h, grp = it["h"], it["grp"]
            ps, pb = psum()
            kn = grp[0][1]
            n = len(grp)
            const = grp[0][2] >= 3
            for j, (sl, _, m) in enumerate(grp):
                S.op("pe", lambda e, j=j, sl=sl, m=m: e.matmul(
                    ps[0:kn, j * nq:(j + 1) * nq], lhsT=BK_sb[:, sl, h // 2, 0:kn], rhs=BQ[:, h, 0:nq],
                    start=True, stop=(m >= 3)), [bbuf[sl], BQ_b], [pb])
                if m < 3:
                    S.op("pe", lambda e, j=j, m=m: e.matmul(
                        ps[0:kn, j * nq:(j + 1) * nq], lhsT=identb[0:kn, 0:kn], rhs=biasb[0:kn, m, h, 0:nq],
                        start=False, stop=True), [WB, Bc], [pb])
            PT, PT_b = wkB["PT"].next()
            it["PT"], it["PT_b"] = PT, PT_b
            if const:
                S.op("act", lambda e: e.activation(
                    out=PT[0:kn, 0:n * nq], in_=ps[0:kn, 0:n * nq], func=AF.Exp, bias=cb[0:kn, h:h + 1]),
                     [pb, WB], [PT_b])
            else:
                S.op("act", lambda e: e.activation(
                    out=PT[0:kn, 0:n * nq], in_=ps[0:kn, 0:n * nq], func=AF.Exp), [pb], [PT_b])
            for j, (sl, _, m) in enumerate(grp):
                if not prompt and sl == 4:
                    S.op("pool", lambda e, j=j: e.memset(PT[64:128, j * nq:(j + 1) * nq], 0.0), [PT_b], [PT_b])
                if prompt and m == 0:
                    S.op("pool", lambda e, j=j: e.memset(PT[64:128, j * nq:j * nq + 64], 0.0), [PT_b], [PT_b])
                if prompt and m == 4:
                    S.op("pool", lambda e, j=j: e.memset(PT[0:64, j * nq + 64:j * nq + 128], 0.0), [PT_b], [PT_b])

        def band_B(it, pso):
            h, grp = it["h"], it["grp"]
            po, pob = pso[h // 4]
            c0 = (h % 4) * 65
            kn = grp[0][1]
            PT, PT_b = it["PT"], it["PT_b"]
            for j, (sl, _, m) in enumerate(grp):
                idx = it["base"] + j
                S.op("pe", lambda e, j=j, sl=sl, st=(idx == 0), sp=(idx == it["nblk"] - 1): e.matmul(
                    po[0:nq, c0:c0 + 65], lhsT=PT[0:kn, j * nq:(j + 1) * nq],
                    rhs=BV_sb[0:kn, sl, h * 65:(h + 1) * 65], start=st, stop=sp), [PT_b, bbuf[sl]], [pob])

        items = []
        for h in range(8):
            base = 0
            for gi, grp in enumerate(bgrps):
                items.append(dict(h=h, grp=grp, gi=gi, base=base, nblk=nblk))
                base += len(grp)
        for i in range(len(items) + 1):
            if i < len(items):
                band_A(items[i])
            if i >= 1:
                band_B(items[i - 1], pso)
        normalize(pso, 512)
        if DBG.get('b0cut') == 3:
            return
        out_proj_ln(Ob, Ob_b, nq, w_out_sb, WB, xr, xr_b, rows0, xmid_d, "xmid", wkB)

    for sd in seq_defs():
        load_seq_B0(sd)
        if sd["kind"] == "p":
            for tq in sd["qtiles"]:
                attn_tile_B0(sd, tq)
        else:
            attn_tile_B0(sd, 0)
    if stop_after <= 1:
        return nc, S

    def ffn_phase(l, src_d, src_key, dst_d, dst_key):
        S.barrier()
        A.reset(P0)
        rot_pool[0] = [2, 3, 4, 5, 6, 7]
        load_lnbc(4 * l + 2)
        WF = Buf("wF")
        wup_sb = A.alloc("wup", [128, 8, 4096], BF16)
        wdn_sb = A.alloc("wdn", [128, 32, 1024], BF16)
        for j in range(8):
            load_wb(wup_sb[:, j, :], wb_up[l, 128 * j:128 * (j + 1), :], f"up{l}", WF)
        for f4 in range(8):
            load_wb(wdn_sb[:, 4 * f4:4 * f4 + 4, :],
                    wb_dn[l, 512 * f4:512 * (f4 + 1), :].rearrange("(c p) n -> p c n", p=128), f"dn{l}", WF)
        xm_r = Ring(A, "fx", [128, 1024], F32, 2)
        xT_r2 = Ring(A, "fxT", [128, 8, 128], BF16, 2)
        hT_r = [(A.alloc(f"hT{i}", [128, 32, 128], BF16), [Buf(f"hT{i}_{g}") for g in range(8)]) for i in range(2)]
        r_r = Ring(A, "fr", [128, 512], F32, 3)
        v_r = Ring(A, "fv", [128, 1024], F32, 2)
        o_r = Ring(A, "fo", [128, 1024], F32, 2)
        t_r = Ring(A, "ft", [128, 1024], F32, 1)
        st_r = Ring(A, "fst", [128, 16], F32, 2)

        def ffn_tile(nt):
            r0 = 128 * nt
            xm, xm_b = xm_r.next()
            S.dma("sp", xm[:, :], src_d[r0:r0 + 128, :], [db(src_key, nt)], [xm_b])
            xT, xT_b = xT_r2.next()
            transposes_f32(xm, xm_b, 128, xT, xT_b, 8)
            hT, hT_bs = hT_r[nt % 2]
            pd = [psum_at(0), psum_at(1)]

            def up(g):
                ps, pb = psum()
                for i in range(4):
                    f = 4 * g + i
                    for j in range(8):
                        S.op("pe", lambda e, ps=ps, i=i, f=f, j=j: e.matmul(
                            ps[:, 128 * i:128 * (i + 1)], lhsT=wup_sb[:, j, 128 * f:128 * (f + 1)], rhs=xT[:, j, :],
                            start=(j == 0), stop=(j == 7)), [WF, xT_b], [pb])
                r, r_b = r_r.next()
                S.op("act", lambda e, ps=ps, r=r: e.activation(out=r[:, :], in_=ps[:, 0:512], func=AF.Relu),
                     [pb], [r_b])
                S.op("pool", lambda e, r=r, g=g: e.tensor_tensor(
                    out=hT[:, 4 * g:4 * g + 4, :].rearrange("p a b -> p (a b)"), in0=r[:, :], in1=r[:, :],
                    op=ALU.mult), [r_b], [hT_bs[g]])

            def down(g):
                for i in range(4):
                    f = 4 * g + i
                    for n in range(2):
                        pt, pbb = pd[n]
                        S.op("pe", lambda e, pt=pt, f=f, n=n: e.matmul(
                            pt[:, 0:512], lhsT=hT[:, f, :], rhs=wdn_sb[:, f, 512 * n:512 * (n + 1)],
                            start=(f == 0), stop=(f == 31)), [hT_bs[g], WF], [pbb])

            up(0)
            for g in range(1, 8):
                up(g)
                down(g - 1)
            down(7)
            v, v_b = v_r.next()
            for n in range(2):
                pt, pbb = pd[n]
                S.op("dve", lambda e, pt=pt, n=n: e.scalar_tensor_tensor(
                    out=v[:, 512 * n:512 * (n + 1)], in0=xm[:, 512 * n:512 * (n + 1)], scalar=ALPHA,
                    in1=pt[:, 0:512], op0=ALU.mult, op1=ALU.add), [pbb, xm_b], [v_b])
            o, o_b = o_r.next()
            tmp, tmp_b = t_r.next()
            st, st_b = st_r.next()
            layer_norm(v, v_b, 128, 0, o, o_b, tmp, tmp_b, st, st_b)
            S.dma("sp", dst_d[r0:r0 + 128, :], o[:, :], [o_b], [db(dst_key, nt)] if dst_key else [])

        for nt in DBG.get("tilesF", range(NT)):
            ffn_tile(nt)

    ffn_phase(0, xmid_d, "xmid", x1_d, "x1")
    if stop_after <= 2:
        return nc, S

    S.barrier()
    A.reset(P0)
    rot_pool[0] = list(range(8))
    WA1 = Buf("wA1")
    w_inc_sb = A.alloc("w_inc", [128, 8, 3072], BF16)
    for j in range(8):
        load_wb(w_inc_sb[:, j, :], wb_in_c[128 * j:128 * (j + 1), :], "in_c", WA1)
    x1_r = Ring(A, "x1t", [128, 1024], F32, 2)
    x1T_r = Ring(A, "x1T", [128, 8, 128], BF16, 2)
    sqb_r = Ring(A, "sqb", [128, 1024], BF16, 2)
    skf_r = Ring(A, "skf", [128, 1024], F32, 3)
    svf_r = Ring(A, "svf", [128, 1024], F32, 3)
    skb_r = Ring(A, "skb", [128, 1024], BF16, 2)
    svb_r = Ring(A, "svb", [128, 1024], BF16, 2)
    sQT_r = Ring(A, "sQT", [128, 8, 128], BF16, 2)
    sKT_r = Ring(A, "sKT", [128, 8, 128], BF16, 2)

    def sb_kv(kf, kf_b, vf, vf_b, blk):
        kb, kb_b = skb_r.next()
        S.op("pool", lambda e: e.tensor_copy(out=kb[:, :], in_=kf[:, :]), [kf_b], [kb_b])
        KT, KT_b = sKT_r.next()
        transposes_bf(lambda i: kb[:, 128 * i:128 * (i + 1)], kb_b, 128, 128, 8,
                      lambda g, m: KT[:, g:g + m, :], KT_b, eng="dve")
        S.dma("sp", skt_d[blk], KT[:, :, :].rearrange("p h n -> p (h n)"), [KT_b], [db("skt", blk)])
        vb, vb_b = svb_r.next()
        S.op("pool", lambda e: e.tensor_copy(out=vb[:, :], in_=vf[:, :]), [vf_b], [vb_b])
        S.dma("sp", sv_d[blk], vb[:, :], [vb_b], [db("sv", blk)])

    def tile_A1(nt):
        r0 = 128 * nt
        xt, xt_b = x1_r.next()
        S.dma("sp", xt[:, :], x1_d[r0:r0 + 128, :], [db("x1", nt)], [xt_b])
        xT, xT_b = x1T_r.next()
        transposes_f32(xt, xt_b, 128, xT, xT_b, 8)
        qb, qb_b = sqb_r.next()
        kf, kf_b = skf_r.next()
        vf, vf_b = svf_r.next()
        for ci in range(6):
            ps, pb = psum()
            for j in range(8):
                S.op("pe", lambda e, ps=ps, j=j, ci=ci: e.matmul(
                    ps[:, 0:512], lhsT=xT[:, j, :], rhs=w_inc_sb[:, j, 512 * ci:512 * (ci + 1)],
                    start=(j == 0), stop=(j == 7)), [xT_b, WA1], [pb])
            c = ci % 2
            if ci < 2:
                S.op("dve", lambda e, ps=ps, c=c: e.tensor_scalar(
                    out=qb[:, 512 * c:512 * (c + 1)], in0=ps[:, 0:512], scalar1=0.125, scalar2=None, op0=ALU.mult),
                     [pb], [qb_b])
            elif ci < 4:
                S.op("act", lambda e, ps=ps, c=c: e.activation(out=kf[:, 512 * c:512 * (c + 1)], in_=ps[:, 0:512],
                                                               func=AF.Copy), [pb], [kf_b])
            else:
                S.op("act", lambda e, ps=ps, c=c: e.activation(out=vf[:, 512 * c:512 * (c + 1)], in_=ps[:, 0:512],
                                                               func=AF.Copy), [pb], [vf_b])
        S.dma("sp", sk_o[r0:r0 + 128, :], kf[:, :], [kf_b], [])
        S.dma("sp", sv_o[r0:r0 + 128, :], vf[:, :], [vf_b], [])
        QT, QT_b = sQT_r.next()
        transposes_bf(lambda i: qb[:, 128 * i:128 * (i + 1)], qb_b, 128, 128, 8,
                      lambda g, m: QT[:, g:g + m, :], QT_b)
        S.dma("sp", sqt_d[nt], QT[:, :, :].rearrange("p h n -> p (h n)"), [QT_b], [db("sqt", nt)])
        sb_kv(kf, kf_b, vf, vf_b, nt)

    def past_sb(sq, j):
        r0 = PAST * sq + 128 * j
        kf, kf_b = skf_r.next()
        vf, vf_b = svf_r.next()
        S.dma("sp", kf[:, :], c_sk[r0:r0 + 128, :], [], [kf_b])
        S.dma("sp", vf[:, :], c_sv[r0:r0 + 128, :], [], [vf_b])
        sb_kv(kf, kf_b, vf, vf_b, 34 + 8 * sq + j)

    for nt in DBG.get('tilesA', range(NT)):
        tile_A1(nt)
    for sq in DBG.get('seqs', range(4)):
        for j in DBG.get('pastj', range(8)):
            past_sb(sq, j)
    if stop_after <= 3:
        return nc, S

    S.barrier()
    A.reset(P0)
    rot_pool[0] = [3, 4, 5, 6, 7]
    load_lnbc(4)
    WB1 = Buf("wB1")
    w_outc_sb = A.alloc("w_outc", [128, 8, 1024], BF16)
    for j in range(8):
        load_wb(w_outc_sb[:, j, :], wb_out_c[128 * j:128 * (j + 1), :], "out_c", WB1)
    SK_sb = A.alloc("SK_sb", [128, 32, 8, 128], BF16)
    SV_sb = A.alloc("SV_sb", [128, 32, 1024], BF16)
    skbuf = [Buf(f"skb{i}") for i in range(32)]
    QG = 2
    wk1 = dict(
        QT=Ring(A, "Q1z", [128, 16, 128 * QG], BF16, 2), xr=Ring(A, "xr1", [128, 1024], F32, 1),
        SP=Ring(A, "SP", [128, 128 * QG], BF16, 5),
        W=Ring(A, "Wt", [128, 128 * QG], BF16, 5), oacc=Ring(A, "oacc", [128, QG, 1024], F32, 1),
        Ob=Ring(A, "Ob1", [128, 1024], BF16, 1), OT=Ring(A, "OT1", [128, 8, 128], BF16, 1),
        v=Ring(A, "v1", [128, 1024], F32, 1), xm=Ring(A, "xm1", [128, 1024], F32, 1),
        st=Ring(A, "st1", [128, 16], F32, 2),
    )
    wk1["tmp"] = wk1["v"]
    for t_, b_ in wk1["QT"].items:
        S.op("pool", lambda e, t_=t_: e.memset(t_[:, :, :], 0.0), [], [b_])

    def load_seq_B1(sd):
        if sd["kind"] == "p":
            for b in range(max(sd["qtiles"]) + 1):
                S.dma("sp", SK_sb[:, b, :, :].rearrange("p h n -> p (h n)"), skt_d[b], [db("skt", b)], [skbuf[b]])
                S.dma("sp", SV_sb[:, b, :], sv_d[b], [db("sv", b)], [skbuf[b]])
        else:
            sq = sd["sq"]
            for j in range(8):
                b = 34 + 8 * sq + j
                S.dma("sp", SK_sb[:, j, :, :].rearrange("p h n -> p (h n)"), skt_d[b], [db("skt", b)], [skbuf[j]])
                S.dma("sp", SV_sb[:, j, :], sv_d[b], [db("sv", b)], [skbuf[j]])
            b = 32 + sq // 2
            off = 64 * (sq % 2)
            S.op("pool", lambda e: e.memset(SK_sb[:, 8, :, 64:128], 0.0), [], [skbuf[8]])
            S.op("pool", lambda e: e.memset(SV_sb[64:128, 8, :], 0.0), [], [skbuf[8]])
            S.dma("sp", SK_sb[:, 8, :, 0:64], skt_d[b].rearrange("p (h n) -> p h n", h=8)[:, :, off:off + 64],
                  [db("skt", b)], [skbuf[8]])
            S.dma("sp", SV_sb[0:64, 8, :], sv_d[b][off:off + 64, :], [db("sv", b)], [skbuf[8]])

    obank = [0]

    def attn_group_B1(sd, qts):
        prompt = sd["kind"] == "p"
        nq = 128 if prompt else 64
        nqt = len(qts)
        NQ = nq * nqt
        if prompt:
            tile0 = qts[0]
            off = 0
        else:
            tile0 = 32 + sd["sq"] // 2
            off = 64 * (sd["sq"] % 2)
        QT, QT_b = wk1["QT"].next()
        QTv = QT[:, :, :].rearrange("p (c two) n -> p c two n", two=2)
        for qi in range(nqt):
            tid = tile0 + qi
            q_src = sqt_d[tid].rearrange("p (h n) -> p h n", h=8)
            S.dma("sp", QTv[0:64, :, 0, nq * qi:nq * (qi + 1)], q_src[0:64, :, off:off + nq],
                  [db("sqt", tid)], [QT_b])
            S.dma("sp", QTv[64:128, :, 1, nq * qi:nq * (qi + 1)], q_src[64:128, :, off:off + nq],
                  [db("sqt", tid)], [QT_b])
        oacc, oacc_b = wk1["oacc"].next()
        if prompt:
            kbl = [(kb, 128 * max(0, kb - tile0), kb >= tile0) for kb in range(tile0 + nqt - 1, -1, -1)]
        else:
            kbl = [(8, 0, True)] + [(j, 0, False) for j in range(7, -1, -1)]
        nkb = len(kbl)

        class U:
            pass
        HG = 4
        units = []
        for hg in range(0, 16, HG):
            ob = obank[0]
            obank[0] ^= 1
            gs = U()
            gs.pO = None
            gs.started = False
            gs.ndone = 0
            last_of = {}
            for i, (kb, c0, dg) in enumerate(kbl):
                for hi in range(HG):
                    u = U()
                    u.h, u.kb, u.c0, u.dg, u.first, u.last, u.hs, u.ob = hg + hi, kb, c0, dg, i == 0, i == nkb - 1, gs, ob
                    u.hi = hi
                    u.gfirst = (i == 0 and hi == 0)
                    u.glast = (i == nkb - 1 and hi == HG - 1)
                    u.prev = last_of.get(hi)
                    last_of[hi] = u
                    u.slot = len(units) % 4
                    units.append(u)

        def S1(u):
            c0 = u.c0
            psA, pAb = psum()
            S.op("pe", lambda e: e.matmul(psA[:, c0:NQ], lhsT=SK_sb[:, u.kb, u.h // 2, :], rhs=QT[:, u.h, c0:NQ],
                                          start=True, stop=True), [skbuf[u.kb], QT_b], [pAb])
            S.op("act", lambda e: e.activation(out=psA[:, c0:NQ], in_=psA[:, c0:NQ], func=AF.Exp), [pAb], [pAb])
            u.SP, u.SP_b = wk1["SP"].next()
            SP = u.SP
            S.op("act", lambda e: e.activation(out=SP[:, c0:NQ], in_=psA[:, c0:NQ], func=AF.Ln, bias=1.0),
                 [pAb], [u.SP_b])
            if u.dg:
                S.op("pool", lambda e: e.affine_select(
                    out=SP[:, c0:c0 + nq], in_=SP[:, c0:c0 + nq], pattern=[[1, nq]], compare_op=ALU.is_gt,
                    fill=0.0, base=0, channel_multiplier=-1), [u.SP_b], [u.SP_b])

        def S2(u):
            c0 = u.c0
            hs = u.hs
            SP = u.SP
            if u.gfirst:
                hs.pO, hs.pOb = psum_at(u.ob)
            if not u.first:
                p = u.prev
                cp = p.c0
                S.op("dve", lambda e: e.tensor_tensor(out=SP[96:128, cp:NQ], in0=SP[96:128, cp:NQ],
                                                      in1=p.pR[96:128, cp:NQ], op=ALU.add),
                     [u.SP_b, p.pRb], [u.SP_b])
            u.psB, u.pBb = psum()
            psB, pBb = u.psB, u.pBb
            S.op("pe", lambda e: e.matmul(psB[:, c0:NQ], lhsT=negtri[:, :], rhs=SP[:, c0:NQ],
                                          start=True, stop=False), [u.SP_b, Bc], [pBb])
            S.op("pe", lambda e: e.matmul(psB[:, c0:NQ], lhsT=SK_sb[:, u.kb, u.h // 2, :], rhs=QT[:, u.h, c0:NQ],
                                          start=False, stop=True), [skbuf[u.kb], QT_b], [pBb])
            if not u.last:
                rt_, u.pRb = pR_banks[u.slot // 2]
                u.pR = rt_[:, 256 * (u.slot % 2):256 * (u.slot % 2) + 256]
                pR = u.pR
                S.op("pe", lambda e: e.matmul(pR[:, c0:NQ], lhsT=e127[:, :], rhs=SP[:, c0:NQ],
                                              start=True, stop=True), [u.SP_b, Bc], [u.pRb])
            u.W, u.W_b = wk1["W"].next()
            Wt = u.W
            S.op("act", lambda e: e.activation(out=Wt[:, c0:NQ], in_=psB[:, c0:NQ], func=AF.Exp), [pBb], [u.W_b])
            if u.dg:
                S.op("pool", lambda e: e.affine_select(
                    out=Wt[:, c0:c0 + nq], in_=Wt[:, c0:c0 + nq], pattern=[[1, nq]], compare_op=ALU.is_gt,
                    fill=0.0, base=0, channel_multiplier=-1), [u.W_b], [u.W_b])

        def S3(u):
            hs = u.hs
            pO, pOb = hs.pO, hs.pOb
            Wt = u.W
            for qi in range(nqt):
                if nq * qi < u.c0:
                    continue
                st = not hs.started
                hs.started = True
                oc = 128 * u.hi + 64 * qi
                S.op("pe", lambda e, qi=qi, st=st, oc=oc: e.matmul(
                    pO[0:nq, oc:oc + 64], lhsT=Wt[:, nq * qi:nq * (qi + 1)],
                    rhs=SV_sb[:, u.kb, 64 * u.h:64 * u.h + 64], start=st, stop=u.glast, skip_group_check=True),
                     [u.W_b, skbuf[u.kb]], [pOb])
            if u.last:
                S.op("dve", lambda e: e.tensor_copy(
                    out=oacc[0:nq, 0:nqt, 64 * u.h:64 * u.h + 64],
                    in_=pO[0:nq, 128 * u.hi:128 * u.hi + 64 * nqt].rearrange("p (q d) -> p q d", q=nqt)),
                     [pOb], [oacc_b])

        n = len(units)
        K2, K3 = 2, 4
        for i in range(n + K3):
            if i < n:
                S1(units[i])
            if 0 <= i - K2 < n:
                S2(units[i - K2])
            if 0 <= i - K3 < n:
                S3(units[i - K3])
        for qi in range(nqt):
            tid = tile0 + qi
            rows0 = 128 * tid + off
            xr, xr_b = wk1["xr"].next()
            S.dma("sp", xr[0:nq, :], x1_d[rows0:rows0 + nq, :], [db("x1", tid)], [xr_b])
            Ob, Ob_b = wk1["Ob"].next()
            S.op("pool", lambda e, qi=qi, Ob=Ob: e.tensor_copy(out=Ob[0:nq, :], in_=oacc[0:nq, qi, :]),
                 [oacc_b], [Ob_b])
            out_proj_ln(Ob, Ob_b, nq, w_outc_sb, WB1, xr, xr_b, rows0, xmid_d, "xmid", wk1)

    rot_pool[0] = [4, 5, 6, 7]
    pR_banks = [psum_at(2), psum_at(3)]
    for sd in seq_defs():
        load_seq_B1(sd)
        if sd["kind"] == "p":
            qts = sd["qtiles"]
            for g in range(0, len(qts), QG):
                attn_group_B1(sd, qts[g:g + QG])
        else:
            attn_group_B1(sd, [0])
    if stop_after <= 4:
        return nc, S
    ffn_phase(1, xmid_d, "xmid", y_o, None)
    return nc, S


def _rope_table():
    half = 16
    inv = np.power(np.float32(10000.0), -np.arange(half, dtype=np.float32) / np.float32(half)).astype(np.float32)
    pos = np.concatenate([np.arange(SEQ), np.tile(PAST + np.arange(64), 4)]).astype(np.float32)
    ang = (pos[:, None] * inv[None, :]).astype(np.float32)
    cos = np.cos(ang).astype(np.float32)
    sin = np.sin(ang).astype(np.float32)
    return np.concatenate([np.tile(cos, (1, 8)), np.tile(sin, (1, 8))], axis=1).astype(np.float32)


def _consts():
    c = np.zeros((128, 5, 128), np.float32)
    c[:, 0, :] = np.eye(128, dtype=np.float32)
    j = np.arange(128)[:, None]
    s = np.arange(128)[None, :]
    c[:, 1, :] = -(j >= s).astype(np.float32)
    c[:, 2, :] = -1.0
    c[:, 3, :] = np.eye(128, dtype=np.float32)[::-1]
    c[:, 4, 127] = 1.0
    return c


STOP_AFTER = 99


def kernel(x_prompt, x_sample, cache_mla_ckv, cache_mla_krope, cache_band_k, cache_band_v,
           cache_sb_k, cache_sb_v, w_in_ab, g_q_lat, w_uq, g_kv_lat, w_ukv, rel_bias, w_out_ab,
           w_in_c, w_out_c, ln_mix_g, ln_mix_b, ln_ffn_g, ln_ffn_b, w_ff_up, w_ff_down):
    from contextlib import ExitStack
    f = lambda a: np.ascontiguousarray(np.asarray(a, dtype=np.float32))
    nc, S = build_program(STOP_AFTER)
    with ExitStack() as stack:
        S.finalize(stack)
    rope_t = _rope_table()
    cst = _consts()
    ln_p = f(np.stack([ln_mix_g[0], ln_mix_b[0], ln_ffn_g[0], ln_ffn_b[0],
                       ln_mix_g[1], ln_mix_b[1], ln_ffn_g[1], ln_ffn_b[1]]))
    shared = {
        "rope": rope_t, "consts": cst,
        "w_in_ab": f(w_in_ab[0]), "g_q": f(np.asarray(g_q_lat[0]).reshape(6, 128).T),
        "w_uq": f(w_uq[0]), "g_kv": f(np.asarray(g_kv_lat[0]).reshape(1, 256)), "w_ukv": f(w_ukv[0]),
        "rel_bias": f(rel_bias[0]), "w_out_ab": f(w_out_ab[0]), "w_in_c": f(w_in_c[0]),
        "w_out_c": f(w_out_c[0]), "ln_p": ln_p, "w_up": f(w_ff_up), "w_dn": f(w_ff_down),
    }
    in_maps = []
    for c in range(8):
        sl = slice(4 * c, 4 * c + 4)
        m = dict(shared)
        m["xin"] = f(np.concatenate([np.asarray(x_prompt[c]), np.asarray(x_sample[sl]).reshape(256, D)], 0))
        m["c_ckv"] = f(np.asarray(cache_mla_ckv[0, sl]).reshape(4 * PAST, 256))
        m["c_kr"] = f(np.asarray(cache_mla_krope[0, sl]).reshape(4 * PAST, 32))
        m["c_bk"] = f(np.asarray(cache_band_k[0, sl]).reshape(4 * 512, 512))
        m["c_bv"] = f(np.asarray(cache_band_v[0, sl]).reshape(4 * 512, 512))
        m["c_sk"] = f(np.asarray(cache_sb_k[0, sl]).reshape(4 * PAST, 1024))
        m["c_sv"] = f(np.asarray(cache_sb_v[0, sl]).reshape(4 * PAST, 1024))
        in_maps.append(m)
    res = run_bass_kernel_spmd(nc, in_maps, core_ids=list(range(8)))
    R = res.results

    def gat(name):
        return np.stack([np.asarray(R[c][name]) for c in range(8)])

    y = gat("y")
    y_p = y[:, :SEQ].reshape(8, SEQ, D)
    y_s = y[:, SEQ:].reshape(32, 64, D)
    ckv = gat("ckv_o")
    kr = gat("kr_o")
    sk = gat("sk_o")
    sv = gat("sv_o")
    outs = (
        y_p, y_s,
        ckv[:, :SEQ].reshape(1, 8, SEQ, 256), kr[:, :SEQ].reshape(1, 8, SEQ, 32),
        gat("bkp_o").reshape(1, 8, 512, 8, 64), gat("bvp_o").reshape(1, 8, 512, 8, 64),
        sk[:, :SEQ].reshape(1, 8, SEQ, 16, 64), sv[:, :SEQ].reshape(1, 8, SEQ, 16, 64),
        ckv[:, SEQ:].reshape(1, 32, 64, 256), kr[:, SEQ:].reshape(1, 32, 64, 32),
        gat("bks_o").reshape(1, 32, 512, 8, 64), gat("bvs_o").reshape(1, 32, 512, 8, 64),
        sk[:, SEQ:].reshape(1, 32, 64, 16, 64), sv[:, SEQ:].reshape(1, 32, 64, 16, 64),
    )
    return tuple(np.ascontiguousarray(o, dtype=np.float32) for o in outs)
```

```python
import numpy as np
import ml_dtypes
import concourse.bass as bass
import concourse.mybir as mybir
from concourse.bass_utils import run_bass_kernel_spmd

F32 = mybir.dt.float32
BF16 = mybir.dt.bfloat16
AF = mybir.ActivationFunctionType
ALU = mybir.AluOpType

D = 1024
SEQ = 4096
NPT = 32
NT = 34
NTOK = NT * 128
PAST = 1024
ALPHA = 4.0 ** 0.25
MLA_SCALE = 96.0 ** -0.5
LN_EPS = 1e-5
RMS_EPS = 1e-6
NMB = 66
NBB = 50


class Buf:
    __slots__ = ("name", "lw", "rd", "dead", "excl")

    def __init__(self, name="", excl=False):
        self.name = name
        self.lw = None
        self.rd = {}
        self.dead = False
        self.excl = excl


class Rec:
    __slots__ = ("eng", "emit", "deps", "sig", "pos", "key", "is_dep", "cclk", "waits",
                 "isdma", "sigidx", "gid")


class Sched:
    CH = 30000
    DCH = 1800

    def __init__(self, nc):
        self.nc = nc
        self.q = {e: [] for e in ("pe", "act", "dve", "pool", "sp")}
        self.all = []
        self.nslots = {"sp": 16, "pool": 8, "act": 4}
        self.slot_rr = {k: 0 for k in self.nslots}
        self.slot_last = {}
        self.slot_cnt = {}
        self._cc = {e: 0 for e in self.q}

    def _add(self, eng, emit, reads, writes, isdma=False):
        r = Rec()
        r.eng = eng
        r.emit = emit
        r.sig = False
        r.is_dep = False
        r.isdma = isdma
        r.cclk = None
        r.gid = len(self.all)
        deps = {}
        for b in reads:
            assert not b.dead, f"read of dead buffer {b.name}"
            if b.lw is not None:
                deps[b.lw.gid] = b.lw
            if b.excl:
                for x in b.rd.values():
                    if x.eng != eng:
                        deps[x.gid] = x
        for b in writes:
            assert not b.dead, f"write of dead buffer {b.name}"
            if b.lw is not None:
                deps[b.lw.gid] = b.lw
            for x in b.rd.values():
                deps[x.gid] = x
        if isdma:
            sl = self.slot_rr[eng]
            self.slot_rr[eng] = (sl + 1) % self.nslots[eng]
            key = ("d", eng, sl)
            prev = self.slot_last.get(key)
            if prev is not None:
                deps[prev.gid] = prev
            self.slot_last[key] = r
            self.slot_cnt[key] = self.slot_cnt.get(key, 0) + 1
            r.key = key
            r.pos = self.slot_cnt[key]
        else:
            r.key = ("c", eng)
            r.pos = self._cc[eng] + 1
        deps.pop(r.gid, None)
        dl = []
        for g in sorted(deps.keys(), reverse=True):
            d = deps[g]
            if (not isdma) and eng == "pe" and d.key == ("c", "pe"):
                continue
            d.is_dep = True
            dl.append(d)
        r.deps = dl
        for b in reads:
            b.rd[r.key] = r
        for b in writes:
            b.lw = r
            b.rd = {}
        self.q[eng].append(r)
        self.all.append(r)
        if not isdma:
            self._cc[eng] = r.pos
        return r

    def op(self, eng, emit, reads=(), writes=()):
        return self._add(eng, emit, reads, writes, False)

    def dma(self, queue, out, in_, reads=(), writes=(), **kw):
        return self._add(queue, lambda e: e.dma_start(out=out, in_=in_, **kw), reads, writes, True)

    def barrier(self):
        lasts = []
        for e, lst in self.q.items():
            for r in reversed(lst):
                if not r.isdma:
                    lasts.append(r)
                    break
        lasts += list(self.slot_last.values())
        bb = Buf("barrier")
        recs = []
        for e in ("pe", "act", "dve", "pool", "sp"):
            r = self._add(e, lambda en: en.nop(), [], [], False)
            dl = {d.gid: d for d in lasts}
            for d in r.deps:
                dl[d.gid] = d
            r.deps = [dl[g] for g in sorted(dl.keys(), reverse=True)]
            for d in r.deps:
                d.is_dep = True
            recs.append(r)
        return recs

    def finalize(self, stack):
        nc = self.nc
        clocks = {e: {} for e in self.q}
        for r in self.all:
            clk = clocks[r.eng]
            waits = []
            for d in r.deps:
                if clk.get(d.key, 0) >= d.pos:
                    continue
                waits.append(d)
                d.sig = True
                if d.cclk:
                    for k, v in d.cclk.items():
                        if clk.get(k, 0) < v:
                            clk[k] = v
                if clk.get(d.key, 0) < d.pos:
                    clk[d.key] = d.pos
            r.waits = waits
            if r.is_dep:
                r.cclk = dict(clk)
        nsig = {}
        for e, lst in self.q.items():
            k = 0
            for r in lst:
                if r.isdma:
                    continue
                if r.sig:
                    r.sigidx = k
                    k += 1
            nsig[e] = k
        self.csems = {}
        for e, k in nsig.items():
            n = max(1, (k + self.CH - 1) // self.CH)
            self.csems[e] = [stack.enter_context(nc.semaphore(f"c_{e}_{i}")) for i in range(n)]
        self.dsems = {}
        for key, cnt in self.slot_cnt.items():
            n = (cnt + self.DCH - 1) // self.DCH
            self.dsems[key] = [stack.enter_context(nc.semaphore(f"d_{key[1]}_{key[2]}_{i}")) for i in range(n)]

        def semval(d):
            if d.isdma:
                k = d.pos - 1
                return self.dsems[d.key][k // self.DCH], 16 * (k % self.DCH + 1)
            k = d.sigidx
            return self.csems[d.eng][k // self.CH], (k % self.CH) + 1

        def emit_engine(ename, e):
            for r in self.q[ename]:
                seen = {}
                for d in r.waits:
                    s, v = semval(d)
                    kk = id(s)
                    if kk not in seen or seen[kk][1] < v:
                        seen[kk] = (s, v)
                for s, v in seen.values():
                    e.wait_ge(s, v)
                ins = r.emit(e)
                if r.isdma:
                    s, v = semval(r)
                    ins.then_inc(s, 16)
                elif r.sig:
                    s, v = semval(r)
                    ins.then_inc(s, 1)
            if ename == "sp":
                for key, r in self.slot_last.items():
                    s, v = semval(r)
                    e.wait_ge(s, v)

        with nc.Block() as block:
            @block.tensor
            def _(e):
                emit_engine("pe", e)

            @block.scalar
            def _(e):
                emit_engine("act", e)

            @block.vector
            def _(e):
                emit_engine("dve", e)

            @block.gpsimd
            def _(e):
                emit_engine("pool", e)

            @block.sync
            def _(e):
                emit_engine("sp", e)


class Arena:
    BASE = 16512
    LIMIT = 229344

    def __init__(self, nc):
        self.nc = nc
        self.off = self.BASE
        self.n = 0

    def alloc(self, name, shape, dt):
        sz = 4 if dt == F32 else 2
        n = 1
        for s in shape[1:]:
            n *= s
        nb = (n * sz + 31) // 32 * 32
        assert self.off + nb <= self.LIMIT, f"SBUF overflow allocating {name}: {self.off}+{nb}"
        self.n += 1
        t = self.nc.alloc_sbuf_tensor_at(f"{name}_{self.n}", list(shape), dt, offset=self.off)
        self.off += nb
        return t

    def mark(self):
        return self.off

    def reset(self, m):
        self.off = m


class Ring:
    def __init__(self, arena, name, shape, dt, n):
        self.items = [(arena.alloc(f"{name}{i}", shape, dt), Buf(f"{name}{i}")) for i in range(n)]
        self.i = 0

    def next(self):
        it = self.items[self.i]
        self.i = (self.i + 1) % len(self.items)
        return it


DBG = {}


def build_program(stop_after=99):
    nc = bass.Bass("TRN2", target_bir_lowering=False)
    S = Sched(nc)
    A = Arena(nc)

    def din(name, shape, dt=F32):
        return nc.dram_tensor(name, list(shape), dt, kind="ExternalInput").ap()

    def dout(name, shape, dt=F32):
        return nc.dram_tensor(name, list(shape), dt, kind="ExternalOutput").ap()

    def dscr(name, shape, dt):
        return nc.dram_tensor(name, list(shape), dt, kind="Internal").ap()

    xin = din("xin", [NTOK, D])
    rope = din("rope", [NTOK, 256])
    c_ckv = din("c_ckv", [4 * PAST, 256])
    c_kr = din("c_kr", [4 * PAST, 32])
    c_bk = din("c_bk", [4 * 512, 512])
    c_bv = din("c_bv", [4 * 512, 512])
    c_sk = din("c_sk", [4 * PAST, 1024])
    c_sv = din("c_sv", [4 * PAST, 1024])
    w_in_ab = din("w_in_ab", [D, 2592])
    g_q = din("g_q", [128, 6])
    w_uq = din("w_uq", [768, 768])
    g_kv = din("g_kv", [1, 256])
    w_ukv = din("w_ukv", [256, 1024])
    rel_bias = din("rel_bias", [8, 320])
    w_out_ab = din("w_out_ab", [1024, 1024])
    w_in_c = din("w_in_c", [D, 3072])
    w_out_c = din("w_out_c", [1024, 1024])
    ln_p = din("ln_p", [8, 1024])
    w_up = din("w_up", [2, D, 4096])
    w_dn = din("w_dn", [2, 4096, D])
    consts = din("consts", [128, 5, 128])
    y_o = dout("y", [NTOK, D])
    ckv_o = dout("ckv_o", [NTOK, 256])
    kr_o = dout("kr_o", [NTOK, 32])
    bkp_o = dout("bkp_o", [512, 512])
    bvp_o = dout("bvp_o", [512, 512])
    bks_o = dout("bks_o", [4 * 512, 512])
    bvs_o = dout("bvs_o", [4 * 512, 512])
    sk_o = dout("sk_o", [NTOK, 1024])
    sv_o = dout("sv_o", [NTOK, 1024])
    qat_d = dscr("qat_d", [NT, 96, 1024], BF16)
    kat_d = dscr("kat_d", [NMB, 96, 1024], BF16)
    va_d = dscr("va_d", [NMB, 128, 520], BF16)
    bqt_d = dscr("bqt_d", [NT, 128, 512], BF16)
    bkt_d = dscr("bkt_d", [NBB, 128, 512], BF16)
    bv_d = dscr("bv_d", [NBB, 128, 520], BF16)
    xmid_d = dscr("xmid_d", [NTOK, D], F32)
    x1_d = dscr("x1_d", [NTOK, D], F32)
    sqt_d = dscr("sqt_d", [NT, 128, 1024], BF16)
    skt_d = dscr("skt_d", [NMB, 128, 1024], BF16)
    sv_d = dscr("sv_d", [NMB, 128, 1024], BF16)
    tp_d = dscr("tp_d", [8, 768], F32)
    wb_out_ab = dscr("wb_out_ab", [1024, 1024], BF16)
    wb_up = dscr("wb_up", [2, D, 4096], BF16)
    wb_dn = dscr("wb_dn", [2, 4096, D], BF16)
    wb_in_c = dscr("wb_in_c", [D, 3072], BF16)
    wb_out_c = dscr("wb_out_c", [1024, 1024], BF16)

    dumps = []

    def dump(name, ap, shape, dt, bufs):
        if not DBG.get('dump'):
            return
        o = nc.dram_tensor("dbg_" + name, list(shape), dt, kind="ExternalOutput").ap()
        S.dma("sp", o, ap, list(bufs), [])
        dumps.append(name)

    Bd = {}

    def db(*k):
        if k not in Bd:
            Bd[k] = Buf(str(k))
        return Bd[k]

    banks = []
    for i in range(8):
        t = nc.alloc_psum_tensor(f"ps{i}", [128, 512], F32)
        banks.append((t, Buf(f"ps{i}", excl=True)))
    bank_i = [0]
    rot_pool = [list(range(8))]

    def psum_at(i):
        t, old = banks[i]
        nb = Buf(old.name, excl=True)
        nb.lw = old.lw
        nb.rd = old.rd
        old.dead = True
        banks[i] = (t, nb)
        return t, nb

    def psum():
        pool = rot_pool[0]
        bank_i[0] = (bank_i[0] + 1) % len(pool)
        return psum_at(pool[bank_i[0]])

    identf = A.alloc("identf", [128, 128], F32)
    identb = A.alloc("identb", [128, 128], BF16)
    negtri = A.alloc("negtri", [128, 128], BF16)
    negones = A.alloc("negones", [128, 128], BF16)
    ones_c = A.alloc("ones_c", [128, 1], BF16)
    Bc = Buf("consts")
    cst = A.alloc("cst", [128, 5, 128], F32)
    e127 = A.alloc("e127", [128, 128], BF16)
    S.dma("sp", cst[:, :, :], consts[:, :, :], [], [Bc])
    S.op("dve", lambda e: e.tensor_copy(out=identf[:, :], in_=cst[:, 0, :]), [Bc], [Bc])
    S.op("dve", lambda e: e.tensor_copy(out=identb[:, :], in_=cst[:, 0, :]), [Bc], [Bc])
    S.op("dve", lambda e: e.tensor_copy(out=negtri[:, :], in_=cst[:, 1, :]), [Bc], [Bc])
    S.op("dve", lambda e: e.tensor_copy(out=negones[:, :], in_=cst[:, 2, :]), [Bc], [Bc])
    S.op("dve", lambda e: e.memset(ones_c[:, :], 1.0), [], [Bc])
    S.op("dve", lambda e: e.tensor_copy(out=e127[:, :], in_=cst[:, 4, :]), [Bc], [Bc])
    P0 = A.mark()
    lnbc_cur = [None, None]

    def load_lnbc(gi):
        t = A.alloc("lnbc", [128, 2, 1024], F32)
        b = Buf("lnbc")
        for i in range(2):
            S.dma("sp", t[:, i, :], ln_p[gi + i:gi + i + 1, :].broadcast_to([128, 1024]), [], [b])
        lnbc_cur[0] = t
        lnbc_cur[1] = b

    def transposes_f32(src_t, src_b, n_rows, dstT, dst_b, nchunks):
        for g in range(0, nchunks, 4):
            ps, pb = psum()
            m = min(4, nchunks - g)
            for i in range(m):
                S.op("pe", lambda e, ps=ps, i=i, g=g: e.transpose(
                    out=ps[:, i * n_rows:(i + 1) * n_rows],
                    in_=src_t[0:n_rows, (g + i) * 128:(g + i + 1) * 128],
                    identity=identf[0:n_rows, 0:n_rows]), [src_b, Bc], [pb])
            S.op("act", lambda e, ps=ps, g=g, m=m: e.copy(
                out=dstT[:, g:g + m, 0:n_rows],
                in_=ps[:, 0:m * n_rows].rearrange("p (c n) -> p c n", c=m)), [pb], [dst_b])

    def transposes_bf(src_ap_fn, src_b, n_rows, width, n, dst_ap_fn, dst_b, eng="act"):
        for g in range(0, n, 8):
            ps, pb = psum()
            pv = ps[:, :].bitcast(BF16)
            m = min(8, n - g)
            for i in range(m):
                S.op("pe", lambda e, pv=pv, i=i, g=g: e.transpose(
                    out=pv[0:width, i * n_rows:(i + 1) * n_rows],
                    in_=src_ap_fn(g + i),
                    identity=identb[0:n_rows, 0:n_rows]), [src_b, Bc], [pb])
            if eng == "act":
                S.op("act", lambda e, pv=pv, g=g, m=m: e.copy(
                    out=dst_ap_fn(g, m),
                    in_=pv[0:width, 0:m * n_rows].rearrange("p (c n) -> p c n", c=m)), [pb], [dst_b])
            else:
                S.op("dve", lambda e, pv=pv, g=g, m=m: e.tensor_copy(
                    out=dst_ap_fn(g, m),
                    in_=pv[0:width, 0:m * n_rows].rearrange("p (c n) -> p c n", c=m)), [pb], [dst_b])

    def layer_norm(v_t, v_b, n_rows, gi, out_t, out_b, tmp_t, tmp_b, st_t, st_b):
        S.op("dve", lambda e: e.bn_stats(out=st_t[0:n_rows, 0:6], in_=v_t[0:n_rows, 0:512]), [v_b], [st_b])
        S.op("dve", lambda e: e.bn_stats(out=st_t[0:n_rows, 6:12], in_=v_t[0:n_rows, 512:1024]), [v_b], [st_b])
        S.op("dve", lambda e: e.bn_aggr(out=st_t[0:n_rows, 12:14], in_=st_t[0:n_rows, 0:12]), [st_b], [st_b])
        S.op("dve", lambda e: e.tensor_scalar(out=st_t[0:n_rows, 14:15], in0=st_t[0:n_rows, 13:14],
                                              scalar1=LN_EPS, scalar2=None, op0=ALU.add), [st_b], [st_b])
        S.op("act", lambda e: e.activation(out=st_t[0:n_rows, 14:15], in_=st_t[0:n_rows, 14:15], func=AF.Ln),
             [st_b], [st_b])
        S.op("act", lambda e: e.activation(out=st_t[0:n_rows, 15:16], in_=st_t[0:n_rows, 14:15], func=AF.Exp,
                                           scale=-0.5), [st_b], [st_b])
        S.op("dve", lambda e: e.tensor_scalar(out=tmp_t[0:n_rows, :], in0=v_t[0:n_rows, :],
                                              scalar1=st_t[0:n_rows, 12:13], scalar2=st_t[0:n_rows, 15:16],
                                              op0=ALU.subtract, op1=ALU.mult), [v_b, st_b], [tmp_b])
        lnbc, lnb = lnbc_cur
        S.op("pool", lambda e: e.tensor_tensor(out=tmp_t[0:n_rows, :], in0=tmp_t[0:n_rows, :],
                                               in1=lnbc[0:n_rows, 0, :], op=ALU.mult), [tmp_b, lnb], [tmp_b])
        S.op("pool", lambda e: e.tensor_tensor(out=out_t[0:n_rows, :], in0=tmp_t[0:n_rows, :],
                                               in1=lnbc[0:n_rows, 1, :], op=ALU.add), [tmp_b, lnb], [out_b])

    def load_w(dst, src_ap, b):
        S.dma("pool", dst, src_ap, [], [b], max_dma_last_dim=4096)

    W = Buf("wA0")
    w_in_sb = A.alloc("w_in", [128, 8, 2592], BF16)
    for j in range(8):
        load_w(w_in_sb[:, j, :], w_in_ab[128 * j:128 * (j + 1), :], W)
    w_uq_f = A.alloc("w_uq_f", [128, 6, 768], F32)
    w_uq_sb = A.alloc("w_uq", [128, 6, 768], BF16)
    gq_sb = A.alloc("gq", [128, 6], F32)
    S.dma("sp", gq_sb[:, :], g_q[:, :], [], [W])
    S.dma("sp", w_uq_f[:, :, :], w_uq.rearrange("(c p) n -> p c n", p=128), [], [W])
    for j in range(6):
        S.op("pool", lambda e, j=j: e.tensor_scalar(out=w_uq_sb[:, j, :], in0=w_uq_f[:, j, :],
                                                    scalar1=gq_sb[:, j:j + 1], scalar2=None, op0=ALU.mult),
             [W], [W])
    w_ukv_sb = A.alloc("w_ukv", [128, 2, 1024], BF16)
    load_w(w_ukv_sb[:, :, :], w_ukv.rearrange("(c p) n -> p c n", p=128), W)
    gkv_bc = A.alloc("gkv_bc", [128, 256], F32)
    S.dma("sp", gkv_bc[:, :], g_kv[0:1, :].broadcast_to([128, 256]), [], [W])

    if not DBG.get("nobg"):
        S.dma("pool", wb_out_ab[:, :], w_out_ab[:, :], [], [db("wbf", "out_ab")], max_dma_last_dim=4096)
        S.dma("pool", wb_up[0], w_up[0], [], [db("wbf", "up0")], max_dma_last_dim=4096)
        S.dma("pool", wb_dn[0], w_dn[0], [], [db("wbf", "dn0")], max_dma_last_dim=4096)
        S.dma("pool", wb_in_c[:, :], w_in_c[:, :], [], [db("wbf", "in_c")], max_dma_last_dim=4096)
        S.dma("pool", wb_out_c[:, :], w_out_c[:, :], [], [db("wbf", "out_c")], max_dma_last_dim=4096)
        S.dma("pool", wb_up[1], w_up[1], [], [db("wbf", "up1")], max_dma_last_dim=4096)
        S.dma("pool", wb_dn[1], w_dn[1], [], [db("wbf", "dn1")], max_dma_last_dim=4096)

    def load_wb(dst, src_ap, key, b):
        S.dma("sp", dst, src_ap, [db("wbf", key)], [b])

    xt_r = Ring(A, "xt", [128, 1024], F32, 2)
    rp_r = Ring(A, "rp", [128, 256], F32, 2)
    xT_r = Ring(A, "xT", [128, 8, 128], BF16, 2)
    hqb_r = Ring(A, "hqb", [128, 768], BF16, 2)
    hqT_r = Ring(A, "hqT", [128, 6, 128], BF16, 2)
    junk = A.alloc("junk", [128, 512], BF16)
    Bjunk = Buf("junk")
    ss_r = Ring(A, "ss", [128, 8], F32, 2)
    bst_r = Ring(A, "bst", [128, 24], F32, 2)
    qf_r = Ring(A, "qf", [128, 8, 96], F32, 2)
    qb_r = Ring(A, "qb", [128, 8, 96], BF16, 2)
    rt_r = Ring(A, "rt", [128, 4, 8, 16], F32, 2)
    QT_r = Ring(A, "QT", [96, 8, 128], BF16, 2)
    ckvf_r = Ring(A, "ckvf", [128, 256], F32, 3)
    ckvb_r = Ring(A, "ckvb", [128, 256], BF16, 2)
    ckvT_r = Ring(A, "ckvT", [128, 2, 128], BF16, 2)
    krf_r = Ring(A, "krf", [128, 32], F32, 3)
    hkvf_r = Ring(A, "hkvf", [128, 256], F32, 2)
    Kf_r = Ring(A, "Kf", [128, 8, 96], BF16, 2)
    KT_r = Ring(A, "KT", [96, 8, 128], BF16, 2)
    Va_r = [(A.alloc(f"Va{i}", [128, 8, 65], BF16), Buf(f"Va{i}")) for i in range(2)]
    for t, b in Va_r:
        S.op("pool", lambda e, t=t: e.memset(t[:, :, 64:65], 1.0), [], [b])
    va_i = [0]
    bqb_r = Ring(A, "bqb", [128, 512], BF16, 2)
    BqT_r = Ring(A, "BqT", [128, 4, 128], BF16, 2)
    bkf_r = Ring(A, "bkf", [128, 512], F32, 3)
    bvf_r = Ring(A, "bvf", [128, 512], F32, 3)
    bkb_r = Ring(A, "bkb", [128, 512], BF16, 2)
    BkT_r = Ring(A, "BkT", [128, 4, 128], BF16, 2)
    Bv_r = [(A.alloc(f"Bv{i}", [128, 8, 65], BF16), Buf(f"Bv{i}")) for i in range(2)]
    for t, b in Bv_r:
        S.op("pool", lambda e, t=t: e.memset(t[:, :, 64:65], 1.0), [], [b])
    bv_i = [0]

    def mla_kv(ckvb, ckvb_b, krf, krf_b, blk):
        ckvT, ckvT_b = ckvT_r.next()
        transposes_bf(lambda i: ckvb[:, 128 * i:128 * (i + 1)], ckvb_b, 128, 128, 2,
                      lambda g, m: ckvT[:, g:g + m, :], ckvT_b)
        Kf, Kf_b = Kf_r.next()
        Vt, Vt_b = Va_r[va_i[0]]
        va_i[0] ^= 1
        for n in range(2):
            ps, pb = psum()
            for j in range(2):
                S.op("pe", lambda e, ps=ps, j=j, n=n: e.matmul(
                    ps[:, 0:512], lhsT=ckvT[:, j, :], rhs=w_ukv_sb[:, j, 512 * n:512 * (n + 1)],
                    start=(j == 0), stop=(j == 1)), [ckvT_b, W], [pb])
            pv = ps[:, 0:512].rearrange("p (h c) -> p h c", h=4)
            S.op("act", lambda e, pv=pv, n=n: e.copy(out=Kf[:, 4 * n:4 * n + 4, 0:64], in_=pv[:, :, 0:64]),
                 [pb], [Kf_b])
            S.op("dve", lambda e, pv=pv, n=n: e.tensor_copy(out=Vt[:, 4 * n:4 * n + 4, 0:64],
                                                            in_=pv[:, :, 64:128]), [pb], [Vt_b])
        S.op("pool", lambda e: e.tensor_copy(out=Kf[:, :, 64:96],
                                             in_=krf[:, 0:32].unsqueeze(1).broadcast_to([128, 8, 32])),
             [krf_b], [Kf_b])
        KT, KT_b = KT_r.next()
        transposes_bf(lambda i: Kf[:, i, :], Kf_b, 128, 96, 8, lambda g, m: KT[:, g:g + m, :], KT_b, eng="dve")
        S.dma("sp", kat_d[blk], KT[:, :, :].rearrange("p h n -> p (h n)"), [KT_b], [db("kat", blk)])
        S.dma("sp", va_d[blk], Vt[:, :, :].rearrange("p h n -> p (h n)"), [Vt_b], [db("va", blk)])

    def band_kv(k_src, k_b, v_src, v_b, blk, from_psum):
        bkb, bkb_b = bkb_r.next()
        S.op("dve", lambda e: e.tensor_copy(out=bkb[:, :], in_=k_src), [k_b], [bkb_b])
        BkT, BkT_b = BkT_r.next()
        transposes_bf(lambda i: bkb[:, 128 * i:128 * (i + 1)], bkb_b, 128, 128, 4,
                      lambda g, m: BkT[:, g:g + m, :], BkT_b)
        S.dma("sp", bkt_d[blk], BkT[:, :, :].rearrange("p h n -> p (h n)"), [BkT_b], [db("bkt", blk)])
        Bv, Bv_b = Bv_r[bv_i[0]]
        bv_i[0] ^= 1
        eng = "dve" if from_psum else "pool"
        S.op(eng, lambda e: e.tensor_copy(out=Bv[:, :, 0:64],
                                          in_=v_src.rearrange("p (h c) -> p h c", h=8)), [v_b], [Bv_b])
        S.dma("sp", bv_d[blk], Bv[:, :, :].rearrange("p h n -> p (h n)"), [Bv_b], [db("bv", blk)])

    CH_A = [(0, 512), (512, 1024), (1024, 1056), (1056, 1568), (1568, 2080), (2080, 2592)]

    def tile_A0(nt):
        r0 = 128 * nt
        xt, xt_b = xt_r.next()
        S.dma("sp", xt[:, :], xin[r0:r0 + 128, :], [], [xt_b])
        rp, rp_b = rp_r.next()
        S.dma("sp", rp[:, :], rope[r0:r0 + 128, :], [], [rp_b])
        xT, xT_b = xT_r.next()
        transposes_f32(xt, xt_b, 128, xT, xT_b, 8)
        if DBG.get('cut', 99) <= 1:
            return
        def chunk(ci):
            c0, c1 = CH_A[ci]
            ps, pb = psum()
            for j in range(8):
                S.op("pe", lambda e, j=j: e.matmul(
                    ps[:, 0:c1 - c0], lhsT=xT[:, j, :], rhs=w_in_sb[:, j, c0:c1],
                    start=(j == 0), stop=(j == 7)), [xT_b, W], [pb])
            return ps, pb
        p0, b0 = chunk(0)
        p1, b1 = chunk(1)
        ss, ss_b = ss_r.next()
        if DBG.get('sub') == 1:
            return
        bst, bst_b = bst_r.next()
        S.op("dve", lambda e: e.bn_stats(out=bst[:, 0:6], in_=p0[:, 0:512]), [b0], [bst_b])
        S.op("dve", lambda e: e.bn_stats(out=bst[:, 6:12], in_=p1[:, 0:256]), [b1], [bst_b])
        S.op("dve", lambda e: e.bn_stats(out=bst[:, 12:18], in_=p1[:, 256:512]), [b1], [bst_b])
        S.op("dve", lambda e: e.bn_aggr(out=bst[:, 18:20], in_=bst[:, 0:12]), [bst_b], [bst_b])
        S.op("dve", lambda e: e.bn_aggr(out=bst[:, 20:22], in_=bst[:, 12:18]), [bst_b], [bst_b])
        S.op("dve", lambda e: e.scalar_tensor_tensor(out=ss[:, 3:4], in0=bst[:, 18:19], scalar=bst[:, 18:19],
                                                     in1=bst[:, 19:20], op0=ALU.mult, op1=ALU.add),
             [bst_b], [ss_b])
        S.op("dve", lambda e: e.scalar_tensor_tensor(out=ss[:, 6:7], in0=bst[:, 20:21], scalar=bst[:, 20:21],
                                                     in1=bst[:, 21:22], op0=ALU.mult, op1=ALU.add),
             [bst_b], [ss_b])
        if DBG.get('sub') == 2:
            return
        hqb, hqb_b = hqb_r.next()
        S.op("dve", lambda e: e.tensor_copy(out=hqb[:, 0:512], in_=p0[:, 0:512]), [b0], [hqb_b])
        S.op("dve", lambda e: e.tensor_copy(out=hqb[:, 512:768], in_=p1[:, 0:256]), [b1], [hqb_b])
        if DBG.get('sub') == 3:
            return
        hkvf, hkvf_b = hkvf_r.next()
        S.op("act", lambda e: e.activation(out=hkvf[:, :], in_=p1[:, 256:512], func=AF.Copy), [b1], [hkvf_b])
        if DBG.get('cut', 99) <= 2:
            return
        p2, b2 = chunk(2)
        rp_k, rp_kb = rp, rp_b
        krf, krf_b = krf_r.next()
        rt2, rt2_b = rt_r.next()
        k1 = rt2[:, 0, 0, :]
        k2 = rt2[:, 1, 0, :]
        k3 = rt2[:, 2, 0, :]
        k4 = rt2[:, 3, 0, :]
        S.op("dve", lambda e: e.tensor_tensor(out=k1, in0=p2[:, 0:16], in1=rp[:, 0:16], op=ALU.mult),
             [b2, rp_b], [rt2_b])
        S.op("dve", lambda e: e.tensor_tensor(out=k2, in0=p2[:, 16:32], in1=rp[:, 128:144], op=ALU.mult),
             [b2, rp_b], [rt2_b])
        S.op("dve", lambda e: e.tensor_tensor(out=k3, in0=p2[:, 16:32], in1=rp[:, 0:16], op=ALU.mult),
             [b2, rp_b], [rt2_b])
        S.op("dve", lambda e: e.tensor_tensor(out=k4, in0=p2[:, 0:16], in1=rp[:, 128:144], op=ALU.mult),
             [b2, rp_b], [rt2_b])
        S.op("dve", lambda e: e.tensor_tensor(out=krf[:, 0:16], in0=k1, in1=k2, op=ALU.subtract),
             [rt2_b], [krf_b])
        S.op("dve", lambda e: e.tensor_tensor(out=krf[:, 16:32], in0=k3, in1=k4, op=ALU.add),
             [rt2_b], [krf_b])
        S.dma("sp", kr_o[r0:r0 + 128, :], krf[:, :], [krf_b], [])
        if DBG.get('cut', 99) <= 3:
            return
        p3, b3 = chunk(3)
        bqb, bqb_b = bqb_r.next()
        S.op("dve", lambda e: e.tensor_scalar(out=bqb[:, :], in0=p3[:, 0:512], scalar1=0.125, scalar2=None,
                                              op0=ALU.mult), [b3], [bqb_b])
        p4, b4 = chunk(4)
        p5, b5 = chunk(5)
        need_out = (nt >= 28)
        if need_out:
            bkf, bkf_b = bkf_r.next()
            bvf, bvf_b = bvf_r.next()
            S.op("act", lambda e: e.activation(out=bkf[:, :], in_=p4[:, 0:512], func=AF.Copy), [b4], [bkf_b])
            S.op("act", lambda e: e.activation(out=bvf[:, :], in_=p5[:, 0:512], func=AF.Copy), [b5], [bvf_b])
            if nt < NPT:
                o0 = 128 * (nt - 28)
                S.dma("sp", bkp_o[o0:o0 + 128, :], bkf[:, :], [bkf_b], [])
                S.dma("sp", bvp_o[o0:o0 + 128, :], bvf[:, :], [bvf_b], [])
            else:
                for hs in range(2):
                    sq = 2 * (nt - NPT) + hs
                    S.dma("sp", bks_o[512 * sq + 448:512 * sq + 512, :], bkf[64 * hs:64 * hs + 64, :], [bkf_b], [])
                    S.dma("sp", bvs_o[512 * sq + 448:512 * sq + 512, :], bvf[64 * hs:64 * hs + 64, :], [bvf_b], [])
        band_kv(p4[:, 0:512], b4, p5[:, 0:512], b5, nt, True)
        BqT, BqT_b = BqT_r.next()
        transposes_bf(lambda i: bqb[:, 128 * i:128 * (i + 1)], bqb_b, 128, 128, 4,
                      lambda g, m: BqT[:, g:g + m, :], BqT_b)
        S.dma("sp", bqt_d[nt], BqT[:, :, :].rearrange("p h n -> p (h n)"), [BqT_b], [db("bqt", nt)])
        if DBG.get('cut', 99) <= 4:
            return
        S.op("dve", lambda e: e.tensor_scalar(out=ss[:, 3:4], in0=ss[:, 3:4], scalar1=RMS_EPS, scalar2=None,
                                              op0=ALU.add), [ss_b], [ss_b])
        S.op("dve", lambda e: e.tensor_scalar(out=ss[:, 6:7], in0=ss[:, 6:7], scalar1=RMS_EPS, scalar2=None,
                                              op0=ALU.add), [ss_b], [ss_b])
        S.op("act", lambda e: e.activation(out=ss[:, 3:4], in_=ss[:, 3:4], func=AF.Ln), [ss_b], [ss_b])
        S.op("act", lambda e: e.activation(out=ss[:, 4:5], in_=ss[:, 3:4], func=AF.Exp, scale=-0.5),
             [ss_b], [ss_b])
        S.op("act", lambda e: e.activation(out=ss[:, 6:7], in_=ss[:, 6:7], func=AF.Ln), [ss_b], [ss_b])
        S.op("act", lambda e: e.activation(out=ss[:, 5:6], in_=ss[:, 6:7], func=AF.Exp, scale=-0.5),
             [ss_b], [ss_b])
        hqT, hqT_b = hqT_r.next()
        transposes_bf(lambda i: hqb[:, 128 * i:128 * (i + 1)], hqb_b, 128, 128, 6,
                      lambda g, m: hqT[:, g:g + m, :], hqT_b)
        qf, qf_b = qf_r.next()
        qf2 = qf[:, :, :].rearrange("p h c -> p (h c)")
        for (n0, n1) in ((0, 512), (512, 768)):
            ps, pb = psum()
            for j in range(6):
                S.op("pe", lambda e, ps=ps, j=j, n0=n0, n1=n1: e.matmul(
                    ps[:, 0:n1 - n0], lhsT=hqT[:, j, :], rhs=w_uq_sb[:, j, n0:n1],
                    start=(j == 0), stop=(j == 5)), [hqT_b, W], [pb])
            S.op("act", lambda e, ps=ps, n0=n0, n1=n1: e.activation(
                out=qf2[:, n0:n1], in_=ps[:, 0:n1 - n0], func=AF.Copy, scale=ss[:, 4:5]), [pb, ss_b], [qf_b])
        qb, qb_b = qb_r.next()
        rt, rt_b = rt_r.next()
        cos8 = rp[:, 0:128].rearrange("p (h c) -> p h c", h=8)
        sin8 = rp[:, 128:256].rearrange("p (h c) -> p h c", h=8)
        S.op("pool", lambda e: e.tensor_copy(out=qb[:, :, 0:64], in_=qf[:, :, 0:64]), [qf_b], [qb_b])
        S.op("dve", lambda e: e.tensor_tensor(out=rt[:, 0], in0=qf[:, :, 64:80], in1=cos8, op=ALU.mult),
             [qf_b, rp_b], [rt_b])
        S.op("dve", lambda e: e.tensor_tensor(out=rt[:, 1], in0=qf[:, :, 80:96], in1=sin8, op=ALU.mult),
             [qf_b, rp_b], [rt_b])
        S.op("dve", lambda e: e.tensor_tensor(out=rt[:, 2], in0=qf[:, :, 80:96], in1=cos8, op=ALU.mult),
             [qf_b, rp_b], [rt_b])
        S.op("dve", lambda e: e.tensor_tensor(out=rt[:, 3], in0=qf[:, :, 64:80], in1=sin8, op=ALU.mult),
             [qf_b, rp_b], [rt_b])
        S.op("dve", lambda e: e.tensor_tensor(out=qb[:, :, 64:80], in0=rt[:, 0], in1=rt[:, 1], op=ALU.subtract),
             [rt_b], [qb_b])
        S.op("dve", lambda e: e.tensor_tensor(out=qb[:, :, 80:96], in0=rt[:, 2], in1=rt[:, 3], op=ALU.add),
             [rt_b], [qb_b])
        QT, QT_b = QT_r.next()
        transposes_bf(lambda i: qb[:, i, :], qb_b, 128, 96, 8, lambda g, m: QT[:, g:g + m, :], QT_b)
        S.dma("sp", qat_d[nt], QT[:, :, :].rearrange("p h n -> p (h n)"), [QT_b], [db("qat", nt)])
        if DBG.get('cut', 99) <= 5:
            return
        ckvf, ckvf_b = ckvf_r.next()
        S.op("dve", lambda e: e.scalar_tensor_tensor(out=ckvf[:, :], in0=hkvf[:, :], scalar=ss[:, 5:6],
                                                     in1=gkv_bc[:, :], op0=ALU.mult, op1=ALU.mult),
             [hkvf_b, ss_b, W], [ckvf_b])
        S.dma("sp", ckv_o[r0:r0 + 128, :], ckvf[:, :], [ckvf_b], [])
        ckvb, ckvb_b = ckvb_r.next()
        S.op("pool", lambda e: e.tensor_copy(out=ckvb[:, :], in_=ckvf[:, :]), [ckvf_b], [ckvb_b])
        mla_kv(ckvb, ckvb_b, krf, krf_b, nt)

    for nt in DBG.get('tilesA', range(NT)):
        tile_A0(nt)

    def past_mla(sq, j):
        r0 = PAST * sq + 128 * j
        ckvf, ckvf_b = ckvf_r.next()
        S.dma("sp", ckvf[:, :], c_ckv[r0:r0 + 128, :], [], [ckvf_b])
        krf, krf_b = krf_r.next()
        S.dma("sp", krf[:, :], c_kr[r0:r0 + 128, :], [], [krf_b])
        ckvb, ckvb_b = ckvb_r.next()
        S.op("pool", lambda e: e.tensor_copy(out=ckvb[:, :], in_=ckvf[:, :]), [ckvf_b], [ckvb_b])
        mla_kv(ckvb, ckvb_b, krf, krf_b, 34 + 8 * sq + j)

    def past_band(sq, j):
        r0 = 512 * sq + 128 * j
        bkf, bkf_b = bkf_r.next()
        bvf, bvf_b = bvf_r.next()
        S.dma("sp", bkf[:, :], c_bk[r0:r0 + 128, :], [], [bkf_b])
        S.dma("sp", bvf[:, :], c_bv[r0:r0 + 128, :], [], [bvf_b])
        band_kv(bkf[:, :], bkf_b, bvf[:, :], bvf_b, 34 + 4 * sq + j, False)

    for sq in DBG.get('seqs', range(4)):
        for j in DBG.get('pastj', range(8)):
            past_mla(sq, j)
        for j in DBG.get('pastjb', range(4)):
            past_band(sq, j)
        S.dma("sp", bks_o[512 * sq:512 * sq + 448, :], c_bk[512 * sq + 64:512 * sq + 512, :], [], [])
        S.dma("sp", bvs_o[512 * sq:512 * sq + 448, :], c_bv[512 * sq + 64:512 * sq + 512, :], [], [])

    if stop_after <= 0:
        return nc, S

    def groups_of(blocks, n=4):
        out = []
        for b in blocks:
            if out and len(out[-1]) < n and out[-1][-1][1] == b[1]:
                out[-1].append(b)
            else:
                out.append([b])
        return out

    def out_proj_ln(Ob, Ob_b, nq, w_sb, w_b, xr, xr_b, rows0, dst_d, dst_key, wk):
        OT, OT_b = wk["OT"].next()
        transposes_bf(lambda i: Ob[0:nq, 128 * i:128 * (i + 1)], Ob_b, nq, 128, 8,
                      lambda g, m: OT[:, g:g + m, 0:nq], OT_b)
        v, v_b = wk["v"].next()
        for n in range(2):
            ps, pb = psum()
            for j in range(8):
                S.op("pe", lambda e, ps=ps, j=j, n=n: e.matmul(
                    ps[0:nq, 0:512], lhsT=OT[:, j, 0:nq], rhs=w_sb[:, j, 512 * n:512 * (n + 1)],
                    start=(j == 0), stop=(j == 7)), [OT_b, w_b], [pb])
            S.op("dve", lambda e, ps=ps, n=n: e.scalar_tensor_tensor(
                out=v[0:nq, 512 * n:512 * (n + 1)], in0=xr[0:nq, 512 * n:512 * (n + 1)], scalar=ALPHA,
                in1=ps[0:nq, 0:512], op0=ALU.mult, op1=ALU.add), [pb, xr_b], [v_b])
        xm, xm_b = wk["xm"].next()
        tmp, tmp_b = wk["tmp"].next()
        st, st_b = wk["st"].next()
        layer_norm(v, v_b, nq, 0, xm, xm_b, tmp, tmp_b, st, st_b)
        S.dma("sp", dst_d[rows0:rows0 + nq, :], xm[0:nq, :], [xm_b], [db(dst_key, rows0 // 128)])

    def seq_defs():
        sd = [dict(kind="p", qtiles=DBG.get("qtiles", list(range(NPT))))]
        for sq in DBG.get("seqs", range(4)):
            sd.append(dict(kind="s", sq=sq))
        if DBG.get("noprompt"):
            sd = sd[1:]
        return sd

    S.barrier()
    A.reset(P0)
    rot_pool[0] = [2, 3, 4, 5, 6, 7]
    load_lnbc(0)
    WB = Buf("wB0")
    w_out_sb = A.alloc("w_out", [128, 8, 1024], BF16)
    for j in range(8):
        load_wb(w_out_sb[:, j, :], wb_out_ab[128 * j:128 * (j + 1), :], "out_ab", WB)
    tpad = A.alloc("tpad", [8, 768], F32)
    rbt = A.alloc("rbt", [8, 320], F32)
    Btp = Buf("tpad")
    S.dma("sp", rbt[:, :], rel_bias[:, :], [], [Btp])
    S.op("dve", lambda e: e.tensor_copy(out=tpad[:, 64:384], in_=rbt[:, :]), [Btp], [Btp])
    S.op("dve", lambda e: e.tensor_copy(out=tpad[:, 0:64], in_=rbt[:, 0:1].broadcast_to([8, 64])), [Btp], [Btp])
    S.op("dve", lambda e: e.tensor_copy(out=tpad[:, 384:768], in_=rbt[:, 319:320].broadcast_to([8, 384])),
         [Btp], [Btp])
    S.dma("sp", tp_d[:, :], tpad[:, :], [Btp], [db("tp")])
    biasf = A.alloc("biasf", [128, 3, 8, 128], F32)
    biasb = A.alloc("biasb", [128, 3, 8, 128], BF16)
    Bbf = Buf("biasf")
    for m in range(3):
        for h in range(8):
            src = bass.AP(tensor=tp_d.tensor, offset=h * 768 + 128 * m, ap=[[1, 128], [1, 128]])
            S.dma("sp", biasf[:, m, h, :], src, [db("tp")], [Bbf])

    def unreverse(m, h4):
        ps, pb = psum()
        for i in range(4):
            S.op("pe", lambda e, i=i: e.matmul(ps[:, 128 * i:128 * (i + 1)], lhsT=cst[:, 3, :],
                                               rhs=biasf[:, m, 4 * h4 + i, :], start=True, stop=True),
                 [Bbf, Bc], [pb])
        S.op("dve", lambda e: e.tensor_copy(out=biasb[:, m, 4 * h4:4 * h4 + 4, :].rearrange("p a b -> p (a b)"),
                                            in_=ps[:, 0:512]), [pb], [WB])
    for m in range(3):
        for h4 in range(2):
            unreverse(m, h4)
    cb = A.alloc("cb", [128, 8], F32)
    S.dma("sp", cb[:, :], bass.AP(tensor=tp_d.tensor, offset=767, ap=[[0, 128], [768, 8]]), [db("tp")], [WB],
          allow_slow_non_contiguous=True)

    if DBG.get('b0cut') == 1:
        return nc, S
    KT_sb = A.alloc("KT_sb", [96, 32, 8, 128], BF16)
    V_sb = A.alloc("V_sb", [128, 32, 520], BF16)
    kbuf = [Buf(f"kb{i}") for i in range(32)]
    BK_sb = A.alloc("BK_sb", [128, 8, 4, 128], BF16)
    BV_sb = A.alloc("BV_sb", [128, 8, 520], BF16)
    bbuf = [Buf(f"bb{i}") for i in range(8)]
    wkB = dict(
        QT=Ring(A, "QTs", [96, 8, 128], BF16, 2), BQ=Ring(A, "BQz", [128, 8, 128], BF16, 2),
        xr=Ring(A, "xr", [128, 1024], F32, 2), PT=Ring(A, "PT", [128, 512], BF16, 3),
        Ob=Ring(A, "Ob", [128, 1024], BF16, 2), rs=Ring(A, "rs", [128, 16], F32, 2),
        OT=Ring(A, "OT", [128, 8, 128], BF16, 2), v=Ring(A, "v", [128, 1024], F32, 1),
        xm=Ring(A, "xm", [128, 1024], F32, 2), tmp=Ring(A, "tmp", [128, 1024], F32, 1),
        st=Ring(A, "st", [128, 16], F32, 2),
    )

    for t_, b_ in wkB["BQ"].items:
        S.op("pool", lambda e, t_=t_: e.memset(t_[:, :, :], 0.0), [], [b_])

    def load_seq_B0(sd):
        if sd["kind"] == "p":
            for b in range(max(sd["qtiles"]) + 1):
                S.dma("sp", KT_sb[:, b, :, :].rearrange("p h n -> p (h n)"), kat_d[b], [db("kat", b)], [kbuf[b]])
                S.dma("sp", V_sb[:, b, :], va_d[b], [db("va", b)], [kbuf[b]])
        else:
            sq = sd["sq"]
            for j in range(8):
                b = 34 + 8 * sq + j
                S.dma("sp", KT_sb[:, j, :, :].rearrange("p h n -> p (h n)"), kat_d[b], [db("kat", b)], [kbuf[j]])
                S.dma("sp", V_sb[:, j, :], va_d[b], [db("va", b)], [kbuf[j]])
            b = 32 + sq // 2
            off = 64 * (sq % 2)
            S.op("pool", lambda e: e.memset(KT_sb[:, 8, :, 64:128], 0.0), [], [kbuf[8]])
            S.op("pool", lambda e: e.memset(V_sb[64:128, 8, :], 0.0), [], [kbuf[8]])
            S.dma("sp", KT_sb[:, 8, :, 0:64], kat_d[b].rearrange("p (h n) -> p h n", h=8)[:, :, off:off + 64],
                  [db("kat", b)], [kbuf[8]])
            S.dma("sp", V_sb[0:64, 8, :], va_d[b][off:off + 64, :], [db("va", b)], [kbuf[8]])
            for j in range(4):
                b = 34 + 4 * sq + j
                S.dma("sp", BK_sb[:, j, :, :].rearrange("p h n -> p (h n)"), bkt_d[b], [db("bkt", b)], [bbuf[j]])
                S.dma("sp", BV_sb[:, j, :], bv_d[b], [db("bv", b)], [bbuf[j]])
            b = 32 + sq // 2
            S.op("pool", lambda e: e.memset(BK_sb[:, 4, :, 64:128], 0.0), [], [bbuf[4]])
            S.op("pool", lambda e: e.memset(BV_sb[64:128, 4, :], 0.0), [], [bbuf[4]])
            S.dma("sp", BK_sb[:, 4, :, 0:64], bkt_d[b].rearrange("p (h n) -> p h n", h=4)[:, :, off:off + 64],
                  [db("bkt", b)], [bbuf[4]])
            S.dma("sp", BV_sb[0:64, 4, :], bv_d[b][off:off + 64, :], [db("bv", b)], [bbuf[4]])

    def attn_tile_B0(sd, tq):
        prompt = sd["kind"] == "p"
        nq = 128 if prompt else 64
        tile_id = tq if prompt else 32 + sd["sq"] // 2
        off = 0 if prompt else 64 * (sd["sq"] % 2)
        rows0 = 128 * tile_id + off
        QT, QT_b = wkB["QT"].next()
        S.dma("sp", QT[:, :, 0:nq], qat_d[tile_id].rearrange("p (h n) -> p h n", h=8)[:, :, off:off + nq],
              [db("qat", tile_id)], [QT_b])
        BQ, BQ_b = wkB["BQ"].next()
        bq_src = bqt_d[tile_id].rearrange("p (h n) -> p h n", h=4)
        BQv = BQ[:, :, :].rearrange("p (c two) n -> p c two n", two=2)
        S.dma("sp", BQv[0:64, :, 0, 0:nq], bq_src[0:64, :, off:off + nq], [db("bqt", tile_id)], [BQ_b])
        S.dma("sp", BQv[64:128, :, 1, 0:nq], bq_src[64:128, :, off:off + nq], [db("bqt", tile_id)], [BQ_b])
        xr, xr_b = wkB["xr"].next()
        S.dma("sp", xr[0:nq, :], xin[rows0:rows0 + nq, :], [], [xr_b])
        if prompt:
            sl = tq % 8
            S.dma("sp", BK_sb[:, sl, :, :].rearrange("p h n -> p (h n)"), bkt_d[tq], [db("bkt", tq)], [bbuf[sl]])
            S.dma("sp", BV_sb[:, sl, :], bv_d[tq], [db("bv", tq)], [bbuf[sl]])
            kblocks = [(b, 128, b == tq) for b in range(tq + 1)]
        else:
            kblocks = [(j, 128, False) for j in range(8)] + [(8, 128, "pad")]
        Ob, Ob_b = wkB["Ob"].next()
        rs, rs_b = wkB["rs"].next()

        def normalize(pso, col0):
            for h in range(8):
                po, pob = pso[h // 4]
                c0 = (h % 4) * 65
                S.op("dve", lambda e, po=po, c0=c0, h=h: e.reciprocal(
                    out=rs[0:nq, col0 // 64 + h:col0 // 64 + h + 1], in_=po[0:nq, c0 + 64:c0 + 65]), [pob], [rs_b])
                S.op("dve", lambda e, po=po, c0=c0, h=h: e.tensor_scalar(
                    out=Ob[0:nq, col0 + 64 * h:col0 + 64 * h + 64], in0=po[0:nq, c0:c0 + 64],
                    scalar1=rs[0:nq, col0 // 64 + h:col0 // 64 + h + 1], scalar2=None, op0=ALU.mult),
                     [pob, rs_b], [Ob_b])

        pso = [psum_at(0), psum_at(1)]
        grps = groups_of(kblocks)
        def mla_A(it):
            h, grp, gi = it["h"], it["grp"], it["gi"]
            ps, pb = psum()
            kn = grp[0][1]
            n = len(grp)
            for j, (b, _, dg) in enumerate(grp):
                S.op("pe", lambda e, j=j, b=b: e.matmul(
                    ps[0:kn, j * nq:(j + 1) * nq], lhsT=KT_sb[:, b, h, 0:kn], rhs=QT[:, h, 0:nq],
                    start=True, stop=True), [kbuf[b], QT_b], [pb])
            PT, PT_b = wkB["PT"].next()
            it["PT"], it["PT_b"] = PT, PT_b
            S.op("act", lambda e: e.activation(
                out=PT[0:kn, 0:n * nq], in_=ps[0:kn, 0:n * nq], func=AF.Exp, scale=MLA_SCALE), [pb], [PT_b])
            for j, (b, _, dg) in enumerate(grp):
                if dg == "pad":
                    S.op("pool", lambda e, j=j: e.memset(PT[64:128, j * nq:(j + 1) * nq], 0.0), [PT_b], [PT_b])
                elif dg:
                    S.op("pool", lambda e, j=j: e.memset(PT[64:128, j * nq:j * nq + 64], 0.0), [PT_b], [PT_b])

        def mla_B(it, pso):
            h, grp = it["h"], it["grp"]
            po, pob = pso[h // 4]
            c0 = (h % 4) * 65
            kn = grp[0][1]
            PT, PT_b = it["PT"], it["PT_b"]
            for j, (b, _, dg) in enumerate(grp):
                idx = it["base"] + j
                S.op("pe", lambda e, j=j, b=b, st=(idx == 0), sp=(idx == it["nblk"] - 1): e.matmul(
                    po[0:nq, c0:c0 + 65], lhsT=PT[0:kn, j * nq:(j + 1) * nq],
                    rhs=V_sb[0:kn, b, h * 65:(h + 1) * 65], start=st, stop=sp), [PT_b, kbuf[b]], [pob])

        items = []
        for h in range(8):
            base = 0
            for gi, grp in enumerate(grps):
                items.append(dict(h=h, grp=grp, gi=gi, base=base, nblk=len(kblocks)))
                base += len(grp)
        for i in range(len(items) + 1):
            if i < len(items):
                mla_A(items[i])
            if i >= 1:
                mla_B(items[i - 1], pso)
        normalize(pso, 0)
        if DBG.get('b0cut') == 2:
            return
        if prompt:
            bl = [(tq - m, 128, m) for m in range(5) if tq - m >= 0]
            gA = [(b % 8, kn, m) for (b, kn, m) in bl if m < 3]
            gB = [(b % 8, kn, m) for (b, kn, m) in bl if m >= 3]
            bgrps = [g for g in (gA, gB) if g]
        else:
            bgrps = [[(0, 128, 3), (1, 128, 3)], [(2, 128, 2), (3, 128, 1), (4, 128, 0)]]
        pso = [psum_at(0), psum_at(1)]
        nblk = sum(len(g) for g in bgrps)
        def band_A(it):
            h, grp = it["h"], it["grp"]
            ps, pb = psum()
            kn = grp[0][1]
            n = len(grp)
            const = grp[0][2] >= 3
            for j, (sl, _, m) in enumerate(grp):
                S.op("pe", lambda e, j=j, sl=sl, m=m: e.matmul(
                    ps[0:kn, j * nq:(j + 1) * nq], lhsT=BK_sb[:, sl, h // 2, 0:kn], rhs=BQ[:, h, 0:nq],
                    start=True, stop=(m >= 3)), [bbuf[sl], BQ_b], [pb])
                if m < 3:
                    S.op("pe", lambda e, j=j, m=m: e.matmul(
                        ps[0:kn, j * nq:(j + 1) * nq], lhsT=identb[0:kn, 0:kn], rhs=biasb[0:kn, m, h, 0:nq],
                        start=False, stop=True), [WB, Bc], [pb])
            PT, PT_b = wkB["PT"].next()
            it["PT"], it["PT_b"] = PT, PT_b
            if const:
                S.op("act", lambda e: e.activation(
                    out=PT[0:kn, 0:n * nq], in_=ps[0:kn, 0:n * nq], func=AF.Exp, bias=cb[0:kn, h:h + 1]),
                     [pb, WB], [PT_b])
            else:
                S.op("act", lambda e: e.activation(
                    out=PT[0:kn, 0:n * nq], in_=ps[0:kn, 0:n * nq], func=AF.Exp), [pb], [PT_b])
            for j, (sl, _, m) in enumerate(grp):
                if not prompt and sl == 4:
                    S.op("pool", lambda e, j=j: e.memset(PT[64:128, j * nq:(j + 1) * nq], 0.0), [PT_b], [PT_b])
                if prompt and m == 0:
                    S.op("pool", lambda e, j=j: e.memset(PT[64:128, j * nq:j * nq + 64], 0.0), [PT_b], [PT_b])
                if prompt and m == 4:
                    S.op("pool", lambda e, j=j: e.memset(PT[0:64, j * nq + 64:j * nq + 128], 0.0), [PT_b], [PT_b])

        def band_B(it, pso):
            h, grp = it["h"], it["grp"]
            po, pob = pso[h // 4]
            c0 = (h % 4) * 65
            kn = grp[0][1]
            PT, PT_b = it["PT"], it["PT_b"]
            for j, (sl, _, m) in enumerate(grp):
                idx = it["base"] + j
                S.op("pe", lambda e, j=j, sl=sl, st=(idx == 0), sp=(idx == it["nblk"] - 1): e.matmul(
                    po[0:nq, c0:c0 + 65], lhsT=PT[0:kn, j * nq:(j + 1) * nq],
                    rhs=BV_sb[0:kn, sl, h * 65:(h + 1) * 65], start=st, stop=sp), [PT_b, bbuf[sl]], [pob])

        items = []
        for h in range(8):
            base = 0
            for gi, grp in enumerate(bgrps):
                items.append(dict(h=h, grp=grp, gi=gi, base=base, nblk=nblk))
                base += len(grp)
        for i in range(len(items) + 1):
            if i < len(items):
                band_A(items[i])
            if i >= 1:
                band_B(items[i - 1], pso)
        normalize(pso, 512)
        if DBG.get('b0cut') == 3:
            return
        out_proj_ln(Ob, Ob_b, nq, w_out_sb, WB, xr, xr_b, rows0, xmid_d, "xmid", wkB)

    for sd in seq_defs():
        load_seq_B0(sd)
        if sd["kind"] == "p":
            for tq in sd["qtiles"]:
                attn_tile_B0(sd, tq)
        else:
            attn_tile_B0(sd, 0)
    if stop_after <= 1:
        return nc, S

    def ffn_phase(l, src_d, src_key, dst_d, dst_key):
        S.barrier()
        A.reset(P0)
        rot_pool[0] = [2, 3, 4, 5, 6, 7]
        load_lnbc(4 * l + 2)
        WF = Buf("wF")
        wup_sb = A.alloc("wup", [128, 8, 4096], BF16)
        wdn_sb = A.alloc("wdn", [128, 32, 1024], BF16)
        for j in range(8):
            load_wb(wup_sb[:, j, :], wb_up[l, 128 * j:128 * (j + 1), :], f"up{l}", WF)
        for f4 in range(8):
            load_wb(wdn_sb[:, 4 * f4:4 * f4 + 4, :],
                    wb_dn[l, 512 * f4:512 * (f4 + 1), :].rearrange("(c p) n -> p c n", p=128), f"dn{l}", WF)
        xm_r = Ring(A, "fx", [128, 1024], F32, 4)
        xT_r2 = Ring(A, "fxT", [128, 8, 256], BF16, 2)
        hT_t = A.alloc("hT", [128, 32, 256], BF16)
        hT_bs = [Buf(f"hT_{g}") for g in range(16)]
        r_r = Ring(A, "fr", [128, 512], F32, 3)
        v_r = Ring(A, "fv", [128, 1024], F32, 1)
        o_r = Ring(A, "fo", [128, 1024], F32, 2)
        st_r = Ring(A, "fst", [128, 16], F32, 2)

        def ffn_macro(nts):
            xms = []
            xT, xT_b = xT_r2.next()
            for t in range(2):
                nt = nts[t]
                r0 = 128 * nt
                xm, xm_b = xm_r.next()
                S.dma("sp", xm[:, :], src_d[r0:r0 + 128, :], [db(src_key, nt)], [xm_b])
                xms.append((xm, xm_b))
                for g in range(0, 8, 4):
                    ps, pb = psum()
                    for i in range(4):
                        S.op("pe", lambda e, ps=ps, i=i, g=g, xm=xm: e.transpose(
                            out=ps[:, i * 128:(i + 1) * 128], in_=xm[:, (g + i) * 128:(g + i + 1) * 128],
                            identity=identf[:, :]), [xm_b, Bc], [pb])
                    S.op("act", lambda e, ps=ps, g=g, t=t: e.copy(
                        out=xT[:, g:g + 4, 128 * t:128 * (t + 1)],
                        in_=ps[:, 0:512].rearrange("p (c n) -> p c n", c=4)), [pb], [xT_b])
            pd = [[psum_at(0), psum_at(1)], [psum_at(2), psum_at(3)]]

            def up(g):
                ps, pb = psum()
                for i in range(2):
                    f = 2 * g + i
                    for j in range(8):
                        S.op("pe", lambda e, i=i, f=f, j=j: e.matmul(
                            ps[:, 256 * i:256 * (i + 1)], lhsT=wup_sb[:, j, 128 * f:128 * (f + 1)],
                            rhs=xT[:, j, :], start=(j == 0), stop=(j == 7)), [WF, xT_b], [pb])
                r, r_b = r_r.next()
                S.op("act", lambda e: e.activation(out=r[:, :], in_=ps[:, 0:512], func=AF.Relu), [pb], [r_b])
                S.op("pool", lambda e: e.tensor_tensor(
                    out=hT_t[:, 2 * g:2 * g + 2, :].rearrange("p a b -> p (a b)"), in0=r[:, :], in1=r[:, :],
                    op=ALU.mult), [r_b], [hT_bs[g]])

            def down(g):
                for i in range(2):
                    f = 2 * g + i
                    for t in range(2):
                        for n in range(2):
                            pt, pbb = pd[t][n]
                            S.op("pe", lambda e, pt=pt, f=f, n=n, t=t: e.matmul(
                                pt[:, 0:512], lhsT=hT_t[:, f, 128 * t:128 * (t + 1)],
                                rhs=wdn_sb[:, f, 512 * n:512 * (n + 1)], start=(f == 0), stop=(f == 31)),
                                 [hT_bs[g], WF], [pbb])

            up(0)
            for g in range(1, 16):
                up(g)
                down(g - 1)
            down(15)
            for t in range(2):
                if t == 1 and nts[0] == nts[1]:
                    continue
                nt = nts[t]
                r0 = 128 * nt
                xm, xm_b = xms[t]
                v, v_b = v_r.next()
                for n in range(2):
                    pt, pbb = pd[t][n]
                    S.op("dve", lambda e, pt=pt, n=n, xm=xm, v=v: e.scalar_tensor_tensor(
                        out=v[:, 512 * n:512 * (n + 1)], in0=xm[:, 512 * n:512 * (n + 1)], scalar=ALPHA,
                        in1=pt[:, 0:512], op0=ALU.mult, op1=ALU.add), [pbb, xm_b], [v_b])
                o, o_b = o_r.next()
                st, st_b = st_r.next()
                layer_norm(v, v_b, 128, 0, o, o_b, v, v_b, st, st_b)
                S.dma("sp", dst_d[r0:r0 + 128, :], o[:, :], [o_b], [db(dst_key, nt)] if dst_key else [])

        tl = list(DBG.get("tilesF", range(NT)))
        rot_pool[0] = [4, 5, 6, 7]
        k = 0
        while k < len(tl):
            if k + 1 < len(tl):
                ffn_macro([tl[k], tl[k + 1]])
                k += 2
            else:
                ffn_macro([tl[k], tl[k]])
                k += 1

    ffn_phase(0, xmid_d, "xmid", x1_d, "x1")
    if stop_after <= 2:
        return nc, S

    S.barrier()
    A.reset(P0)
    rot_pool[0] = list(range(8))
    WA1 = Buf("wA1")
    w_inc_sb = A.alloc("w_inc", [128, 8, 3072], BF16)
    for j in range(8):
        load_wb(w_inc_sb[:, j, :], wb_in_c[128 * j:128 * (j + 1), :], "in_c", WA1)
    x1_r = Ring(A, "x1t", [128, 1024], F32, 2)
    x1T_r = Ring(A, "x1T", [128, 8, 128], BF16, 2)
    sqb_r = Ring(A, "sqb", [128, 1024], BF16, 2)
    skf_r = Ring(A, "skf", [128, 1024], F32, 3)
    svf_r = Ring(A, "svf", [128, 1024], F32, 3)
    skb_r = Ring(A, "skb", [128, 1024], BF16, 2)
    svb_r = Ring(A, "svb", [128, 1024], BF16, 2)
    sQT_r = Ring(A, "sQT", [128, 8, 128], BF16, 2)
    sKT_r = Ring(A, "sKT", [128, 8, 128], BF16, 2)

    def sb_kv(kf, kf_b, vf, vf_b, blk):
        kb, kb_b = skb_r.next()
        S.op("pool", lambda e: e.tensor_copy(out=kb[:, :], in_=kf[:, :]), [kf_b], [kb_b])
        KT, KT_b = sKT_r.next()
        transposes_bf(lambda i: kb[:, 128 * i:128 * (i + 1)], kb_b, 128, 128, 8,
                      lambda g, m: KT[:, g:g + m, :], KT_b, eng="dve")
        S.dma("sp", skt_d[blk], KT[:, :, :].rearrange("p h n -> p (h n)"), [KT_b], [db("skt", blk)])
        vb, vb_b = svb_r.next()
        S.op("pool", lambda e: e.tensor_copy(out=vb[:, :], in_=vf[:, :]), [vf_b], [vb_b])
        S.dma("sp", sv_d[blk], vb[:, :], [vb_b], [db("sv", blk)])

    def tile_A1(nt):
        r0 = 128 * nt
        xt, xt_b = x1_r.next()
        S.dma("sp", xt[:, :], x1_d[r0:r0 + 128, :], [db("x1", nt)], [xt_b])
        xT, xT_b = x1T_r.next()
        transposes_f32(xt, xt_b, 128, xT, xT_b, 8)
        qb, qb_b = sqb_r.next()
        kf, kf_b = skf_r.next()
        vf, vf_b = svf_r.next()
        for ci in range(6):
            ps, pb = psum()
            for j in range(8):
                S.op("pe", lambda e, ps=ps, j=j, ci=ci: e.matmul(
                    ps[:, 0:512], lhsT=xT[:, j, :], rhs=w_inc_sb[:, j, 512 * ci:512 * (ci + 1)],
                    start=(j == 0), stop=(j == 7)), [xT_b, WA1], [pb])
            c = ci % 2
            if ci < 2:
                S.op("dve", lambda e, ps=ps, c=c: e.tensor_scalar(
                    out=qb[:, 512 * c:512 * (c + 1)], in0=ps[:, 0:512], scalar1=0.125, scalar2=None, op0=ALU.mult),
                     [pb], [qb_b])
            elif ci < 4:
                S.op("act", lambda e, ps=ps, c=c: e.activation(out=kf[:, 512 * c:512 * (c + 1)], in_=ps[:, 0:512],
                                                               func=AF.Copy), [pb], [kf_b])
            else:
                S.op("act", lambda e, ps=ps, c=c: e.activation(out=vf[:, 512 * c:512 * (c + 1)], in_=ps[:, 0:512],
                                                               func=AF.Copy), [pb], [vf_b])
        S.dma("sp", sk_o[r0:r0 + 128, :], kf[:, :], [kf_b], [])
        S.dma("sp", sv_o[r0:r0 + 128, :], vf[:, :], [vf_b], [])
        QT, QT_b = sQT_r.next()
        transposes_bf(lambda i: qb[:, 128 * i:128 * (i + 1)], qb_b, 128, 128, 8,
                      lambda g, m: QT[:, g:g + m, :], QT_b)
        S.dma("sp", sqt_d[nt], QT[:, :, :].rearrange("p h n -> p (h n)"), [QT_b], [db("sqt", nt)])
        sb_kv(kf, kf_b, vf, vf_b, nt)

    def past_sb(sq, j):
        r0 = PAST * sq + 128 * j
        kf, kf_b = skf_r.next()
        vf, vf_b = svf_r.next()
        S.dma("sp", kf[:, :], c_sk[r0:r0 + 128, :], [], [kf_b])
        S.dma("sp", vf[:, :], c_sv[r0:r0 + 128, :], [], [vf_b])
        sb_kv(kf, kf_b, vf, vf_b, 34 + 8 * sq + j)

    for nt in DBG.get('tilesA', range(NT)):
        tile_A1(nt)
    for sq in DBG.get('seqs', range(4)):
        for j in DBG.get('pastj', range(8)):
            past_sb(sq, j)
    if stop_after <= 3:
        return nc, S

    S.barrier()
    A.reset(P0)
    rot_pool[0] = [3, 4, 5, 6, 7]
    load_lnbc(4)
    WB1 = Buf("wB1")
    w_outc_sb = A.alloc("w_outc", [128, 8, 1024], BF16)
    for j in range(8):
        load_wb(w_outc_sb[:, j, :], wb_out_c[128 * j:128 * (j + 1), :], "out_c", WB1)
    SK_sb = A.alloc("SK_sb", [128, 32, 8, 128], BF16)
    SV_sb = A.alloc("SV_sb", [128, 32, 1024], BF16)
    skbuf = [Buf(f"skb{i}") for i in range(32)]
    QG = 2
    wk1 = dict(
        QT=Ring(A, "Q1z", [128, 16, 128 * QG], BF16, 2), xr=Ring(A, "xr1", [128, 1024], F32, 1),
        SP=Ring(A, "SP", [128, 128 * QG], BF16, 5),
        W=Ring(A, "Wt", [128, 128 * QG], BF16, 5), oacc=Ring(A, "oacc", [128, QG, 1024], F32, 1),
        Ob=Ring(A, "Ob1", [128, 1024], BF16, 1), OT=Ring(A, "OT1", [128, 8, 128], BF16, 1),
        v=Ring(A, "v1", [128, 1024], F32, 1), xm=Ring(A, "xm1", [128, 1024], F32, 1),
        st=Ring(A, "st1", [128, 16], F32, 2),
    )
    wk1["tmp"] = wk1["v"]
    for t_, b_ in wk1["QT"].items:
        S.op("pool", lambda e, t_=t_: e.memset(t_[:, :, :], 0.0), [], [b_])

    def load_seq_B1(sd):
        if sd["kind"] == "p":
            for b in range(max(sd["qtiles"]) + 1):
                S.dma("sp", SK_sb[:, b, :, :].rearrange("p h n -> p (h n)"), skt_d[b], [db("skt", b)], [skbuf[b]])
                S.dma("sp", SV_sb[:, b, :], sv_d[b], [db("sv", b)], [skbuf[b]])
        else:
            sq = sd["sq"]
            for j in range(8):
                b = 34 + 8 * sq + j
                S.dma("sp", SK_sb[:, j, :, :].rearrange("p h n -> p (h n)"), skt_d[b], [db("skt", b)], [skbuf[j]])
                S.dma("sp", SV_sb[:, j, :], sv_d[b], [db("sv", b)], [skbuf[j]])
            b = 32 + sq // 2
            off = 64 * (sq % 2)
            S.op("pool", lambda e: e.memset(SK_sb[:, 8, :, 64:128], 0.0), [], [skbuf[8]])
            S.op("pool", lambda e: e.memset(SV_sb[64:128, 8, :], 0.0), [], [skbuf[8]])
            S.dma("sp", SK_sb[:, 8, :, 0:64], skt_d[b].rearrange("p (h n) -> p h n", h=8)[:, :, off:off + 64],
                  [db("skt", b)], [skbuf[8]])
            S.dma("sp", SV_sb[0:64, 8, :], sv_d[b][off:off + 64, :], [db("sv", b)], [skbuf[8]])

    obank = [0]

    def attn_group_B1(sd, qts):
        prompt = sd["kind"] == "p"
        nq = 128 if prompt else 64
        nqt = len(qts)
        NQ = nq * nqt
        if prompt:
            tile0 = qts[0]
            off = 0
        else:
            tile0 = 32 + sd["sq"] // 2
            off = 64 * (sd["sq"] % 2)
        QT, QT_b = wk1["QT"].next()
        QTv = QT[:, :, :].rearrange("p (c two) n -> p c two n", two=2)
        for qi in range(nqt):
            tid = tile0 + qi
            q_src = sqt_d[tid].rearrange("p (h n) -> p h n", h=8)
            S.dma("sp", QTv[0:64, :, 0, nq * qi:nq * (qi + 1)], q_src[0:64, :, off:off + nq],
                  [db("sqt", tid)], [QT_b])
            S.dma("sp", QTv[64:128, :, 1, nq * qi:nq * (qi + 1)], q_src[64:128, :, off:off + nq],
                  [db("sqt", tid)], [QT_b])
        oacc, oacc_b = wk1["oacc"].next()
        if prompt:
            kbl = [(kb, 128 * max(0, kb - tile0), kb >= tile0) for kb in range(tile0 + nqt - 1, -1, -1)]
        else:
            kbl = [(8, 0, True)] + [(j, 0, False) for j in range(7, -1, -1)]
        nkb = len(kbl)

        class U:
            pass
        HG = 4
        units = []
        for hg in range(0, 16, HG):
            ob = obank[0]
            obank[0] ^= 1
            gs = U()
            gs.pO = None
            gs.started = False
            gs.ndone = 0
            last_of = {}
            for i, (kb, c0, dg) in enumerate(kbl):
                for hi in range(HG):
                    u = U()
                    u.h, u.kb, u.c0, u.dg, u.first, u.last, u.hs, u.ob = hg + hi, kb, c0, dg, i == 0, i == nkb - 1, gs, ob
                    u.hi = hi
                    u.gfirst = (i == 0 and hi == 0)
                    u.glast = (i == nkb - 1 and hi == HG - 1)
                    u.prev = last_of.get(hi)
                    last_of[hi] = u
                    u.slot = len(units) % 4
                    units.append(u)

        def S1(u):
            c0 = u.c0
            psA, pAb = psum()
            S.op("pe", lambda e: e.matmul(psA[:, c0:NQ], lhsT=SK_sb[:, u.kb, u.h // 2, :], rhs=QT[:, u.h, c0:NQ],
                                          start=True, stop=True), [skbuf[u.kb], QT_b], [pAb])
            S.op("act", lambda e: e.activation(out=psA[:, c0:NQ], in_=psA[:, c0:NQ], func=AF.Exp), [pAb], [pAb])
            u.SP, u.SP_b = wk1["SP"].next()
            SP = u.SP
            S.op("act", lambda e: e.activation(out=SP[:, c0:NQ], in_=psA[:, c0:NQ], func=AF.Ln, bias=1.0),
                 [pAb], [u.SP_b])
            if u.dg:
                S.op("pool", lambda e: e.affine_select(
                    out=SP[:, c0:c0 + nq], in_=SP[:, c0:c0 + nq], pattern=[[1, nq]], compare_op=ALU.is_gt,
                    fill=0.0, base=0, channel_multiplier=-1), [u.SP_b], [u.SP_b])

        def S2(u):
            c0 = u.c0
            hs = u.hs
            SP = u.SP
            if u.gfirst:
                hs.pO, hs.pOb = psum_at(u.ob)
            if not u.first:
                p = u.prev
                cp = p.c0
                S.op("dve", lambda e: e.tensor_tensor(out=SP[96:128, cp:NQ], in0=SP[96:128, cp:NQ],
                                                      in1=p.pR[96:128, cp:NQ], op=ALU.add),
                     [u.SP_b, p.pRb], [u.SP_b])
            u.psB, u.pBb = psum()
            psB, pBb = u.psB, u.pBb
            S.op("pe", lambda e: e.matmul(psB[:, c0:NQ], lhsT=negtri[:, :], rhs=SP[:, c0:NQ],
                                          start=True, stop=False), [u.SP_b, Bc], [pBb])
            S.op("pe", lambda e: e.matmul(psB[:, c0:NQ], lhsT=SK_sb[:, u.kb, u.h // 2, :], rhs=QT[:, u.h, c0:NQ],
                                          start=False, stop=True), [skbuf[u.kb], QT_b], [pBb])
            if not u.last:
                rt_, u.pRb = pR_banks[u.slot // 2]
                u.pR = rt_[:, 256 * (u.slot % 2):256 * (u.slot % 2) + 256]
                pR = u.pR
                S.op("pe", lambda e: e.matmul(pR[:, c0:NQ], lhsT=e127[:, :], rhs=SP[:, c0:NQ],
                                              start=True, stop=True), [u.SP_b, Bc], [u.pRb])
            u.W, u.W_b = wk1["W"].next()
            Wt = u.W
            S.op("act", lambda e: e.activation(out=Wt[:, c0:NQ], in_=psB[:, c0:NQ], func=AF.Exp), [pBb], [u.W_b])
            if u.dg:
                S.op("pool", lambda e: e.affine_select(
                    out=Wt[:, c0:c0 + nq], in_=Wt[:, c0:c0 + nq], pattern=[[1, nq]], compare_op=ALU.is_gt,
                    fill=0.0, base=0, channel_multiplier=-1), [u.W_b], [u.W_b])

        def S3(u):
            hs = u.hs
            pO, pOb = hs.pO, hs.pOb
            Wt = u.W
            for qi in range(nqt):
                if nq * qi < u.c0:
                    continue
                st = not hs.started
                hs.started = True
                oc = 128 * u.hi + 64 * qi
                S.op("pe", lambda e, qi=qi, st=st, oc=oc: e.matmul(
                    pO[0:nq, oc:oc + 64], lhsT=Wt[:, nq * qi:nq * (qi + 1)],
                    rhs=SV_sb[:, u.kb, 64 * u.h:64 * u.h + 64], start=st, stop=u.glast, skip_group_check=True),
                     [u.W_b, skbuf[u.kb]], [pOb])
            if u.last:
                S.op("dve", lambda e: e.tensor_copy(
                    out=oacc[0:nq, 0:nqt, 64 * u.h:64 * u.h + 64],
                    in_=pO[0:nq, 128 * u.hi:128 * u.hi + 64 * nqt].rearrange("p (q d) -> p q d", q=nqt)),
                     [pOb], [oacc_b])

        n = len(units)
        K2, K3 = 2, 4
        for i in range(n + K3):
            if i < n:
                S1(units[i])
            if 0 <= i - K2 < n:
                S2(units[i - K2])
            if 0 <= i - K3 < n:
                S3(units[i - K3])
        for qi in range(nqt):
            tid = tile0 + qi
            rows0 = 128 * tid + off
            xr, xr_b = wk1["xr"].next()
            S.dma("sp", xr[0:nq, :], x1_d[rows0:rows0 + nq, :], [db("x1", tid)], [xr_b])
            Ob, Ob_b = wk1["Ob"].next()
            S.op("pool", lambda e, qi=qi, Ob=Ob: e.tensor_copy(out=Ob[0:nq, :], in_=oacc[0:nq, qi, :]),
                 [oacc_b], [Ob_b])
            out_proj_ln(Ob, Ob_b, nq, w_outc_sb, WB1, xr, xr_b, rows0, xmid_d, "xmid", wk1)

    rot_pool[0] = [4, 5, 6, 7]
    pR_banks = [psum_at(2), psum_at(3)]
    for sd in seq_defs():
        load_seq_B1(sd)
        if sd["kind"] == "p":
            qts = sd["qtiles"]
            for g in range(0, len(qts), QG):
                attn_group_B1(sd, qts[g:g + QG])
        else:
            attn_group_B1(sd, [0])
    if stop_after <= 4:
        return nc, S
    ffn_phase(1, xmid_d, "xmid", y_o, None)
    return nc, S


def _rope_table():
    half = 16
    inv = np.power(np.float32(10000.0), -np.arange(half, dtype=np.float32) / np.float32(half)).astype(np.float32)
    pos = np.concatenate([np.arange(SEQ), np.tile(PAST + np.arange(64), 4)]).astype(np.float32)
    ang = (pos[:, None] * inv[None, :]).astype(np.float32)
    cos = np.cos(ang).astype(np.float32)
    sin = np.sin(ang).astype(np.float32)
    return np.concatenate([np.tile(cos, (1, 8)), np.tile(sin, (1, 8))], axis=1).astype(np.float32)


def _consts():
    c = np.zeros((128, 5, 128), np.float32)
    c[:, 0, :] = np.eye(128, dtype=np.float32)
    j = np.arange(128)[:, None]
    s = np.arange(128)[None, :]
    c[:, 1, :] = -(j >= s).astype(np.float32)
    c[:, 2, :] = -1.0
    c[:, 3, :] = np.eye(128, dtype=np.float32)[::-1]
    c[:, 4, 127] = 1.0
    return c


STOP_AFTER = 99


def kernel(x_prompt, x_sample, cache_mla_ckv, cache_mla_krope, cache_band_k, cache_band_v,
           cache_sb_k, cache_sb_v, w_in_ab, g_q_lat, w_uq, g_kv_lat, w_ukv, rel_bias, w_out_ab,
           w_in_c, w_out_c, ln_mix_g, ln_mix_b, ln_ffn_g, ln_ffn_b, w_ff_up, w_ff_down):
    from contextlib import ExitStack
    f = lambda a: np.ascontiguousarray(np.asarray(a, dtype=np.float32))
    nc, S = build_program(STOP_AFTER)
    with ExitStack() as stack:
        S.finalize(stack)
    rope_t = _rope_table()
    cst = _consts()
    ln_p = f(np.stack([ln_mix_g[0], ln_mix_b[0], ln_ffn_g[0], ln_ffn_b[0],
                       ln_mix_g[1], ln_mix_b[1], ln_ffn_g[1], ln_ffn_b[1]]))
    shared = {
        "rope": rope_t, "consts": cst,
        "w_in_ab": f(w_in_ab[0]), "g_q": f(np.asarray(g_q_lat[0]).reshape(6, 128).T),
        "w_uq": f(w_uq[0]), "g_kv": f(np.asarray(g_kv_lat[0]).reshape(1, 256)), "w_ukv": f(w_ukv[0]),
        "rel_bias": f(rel_bias[0]), "w_out_ab": f(w_out_ab[0]), "w_in_c": f(w_in_c[0]),
        "w_out_c": f(w_out_c[0]), "ln_p": ln_p, "w_up": f(w_ff_up), "w_dn": f(w_ff_down),
    }
    in_maps = []
    for c in range(8):
        sl = slice(4 * c, 4 * c + 4)
        m = dict(shared)
        m["xin"] = f(np.concatenate([np.asarray(x_prompt[c]), np.asarray(x_sample[sl]).reshape(256, D)], 0))
        m["c_ckv"] = f(np.asarray(cache_mla_ckv[0, sl]).reshape(4 * PAST, 256))
        m["c_kr"] = f(np.asarray(cache_mla_krope[0, sl]).reshape(4 * PAST, 32))
        m["c_bk"] = f(np.asarray(cache_band_k[0, sl]).reshape(4 * 512, 512))
        m["c_bv"] = f(np.asarray(cache_band_v[0, sl]).reshape(4 * 512, 512))
        m["c_sk"] = f(np.asarray(cache_sb_k[0, sl]).reshape(4 * PAST, 1024))
        m["c_sv"] = f(np.asarray(cache_sb_v[0, sl]).reshape(4 * PAST, 1024))
        in_maps.append(m)
    res = run_bass_kernel_spmd(nc, in_maps, core_ids=list(range(8)))
    R = res.results

    def gat(name):
        return np.stack([np.asarray(R[c][name]) for c in range(8)])

    y = gat("y")
    y_p = y[:, :SEQ].reshape(8, SEQ, D)
    y_s = y[:, SEQ:].reshape(32, 64, D)
    ckv = gat("ckv_o")
    kr = gat("kr_o")
    sk = gat("sk_o")
    sv = gat("sv_o")
    outs = (
        y_p, y_s,
        ckv[:, :SEQ].reshape(1, 8, SEQ, 256), kr[:, :SEQ].reshape(1, 8, SEQ, 32),
        gat("bkp_o").reshape(1, 8, 512, 8, 64), gat("bvp_o").reshape(1, 8, 512, 8, 64),
        sk[:, :SEQ].reshape(1, 8, SEQ, 16, 64), sv[:, :SEQ].reshape(1, 8, SEQ, 16, 64),
        ckv[:, SEQ:].reshape(1, 32, 64, 256), kr[:, SEQ:].reshape(1, 32, 64, 32),
        gat("bks_o").reshape(1, 32, 512, 8, 64), gat("bvs_o").reshape(1, 32, 512, 8, 64),
        sk[:, SEQ:].reshape(1, 32, 64, 16, 64), sv[:, SEQ:].reshape(1, 32, 64, 16, 64),
    )
    return tuple(np.ascontiguousarray(o, dtype=np.float32) for o in outs)
```
